# Optimizing a Trainium2 kernel written in Bass

```python
import jax, jax.numpy as jnp
from jax import lax
import numpy as np

D_MODEL = 1024
BATCH = 2
SEQ = 8192
DEPTH = 1

HEAD_DIM = 64
D_MIX = D_MODEL
N_HEADS_A = (D_MIX // 2) // HEAD_DIM
N_KV_A = 2
N_HEADS_B = (D_MIX // 2) // HEAD_DIM
DILATED_CONFIGS = ((128, 1), (512, 4), (2048, 16))
Q_BLOCK = 128
GRID_W = 64
ROPE_THETA = 10000.0
D_FF = 2816
EPS = 1e-6
NEG_INF = -1e30

A_Q = N_HEADS_A * HEAD_DIM
A_KV = N_KV_A * HEAD_DIM
B_QKV = N_HEADS_B * HEAD_DIM
QKV_COLS = A_Q + 2 * A_KV + 3 * B_QKV
QKV_SPLITS = (A_Q, A_Q + A_KV, A_Q + 2 * A_KV, A_Q + 2 * A_KV + B_QKV, A_Q + 2 * A_KV + 2 * B_QKV)

kernel_name = "hybrid_gqa_axial_dilated_macaron_block"


def _rmsnorm(x, g):
    xf = x.astype(jnp.float32)
    r = lax.rsqrt(jnp.mean(xf * xf, axis=-1, keepdims=True) + EPS)
    return (xf * r * g.astype(jnp.float32)).astype(x.dtype)


def _rope_angles(pos, dim):
    freqs = ROPE_THETA ** (-jnp.arange(0, dim, 2, dtype=jnp.float32) / dim)
    return pos.astype(jnp.float32)[:, None] * freqs[None, :]


def _apply_rope(x, ang):
    xf = x.astype(jnp.float32)
    x1, x2 = jnp.split(xf, 2, axis=-1)
    cos = jnp.cos(ang)[None, :, None, :]
    sin = jnp.sin(ang)[None, :, None, :]
    return jnp.concatenate([x1 * cos - x2 * sin, x2 * cos + x1 * sin], axis=-1).astype(x.dtype)


def _apply_axial_rope(x, ang_row, ang_col):
    xr, xc = jnp.split(x, 2, axis=-1)
    return jnp.concatenate([_apply_rope(xr, ang_row), _apply_rope(xc, ang_col)], axis=-1)


def _swiglu(x, w_gate, w_up, w_down):
    return (jax.nn.silu(x @ w_gate) * (x @ w_up)) @ w_down


def _gqa_dense_blocked(q, k, v):
    B, S, Hq, D = q.shape
    Hkv = k.shape[2]
    G = Hq // Hkv
    nq = S // Q_BLOCK
    scale = D ** -0.5
    qb = q.reshape(B, nq, Q_BLOCK, Hkv, G, D).transpose(1, 0, 2, 3, 4, 5)

    def block(qi):
        s = jnp.einsum('bqkgd,bskd->bkgqs', qi, k, preferred_element_type=jnp.float32) * scale
        p = jax.nn.softmax(s, axis=-1)
        o = jnp.einsum('bkgqs,bskd->bqkgd', p.astype(v.dtype), v, preferred_element_type=jnp.float32)
        return o.astype(v.dtype)

    o = lax.map(block, qb)
    return o.transpose(1, 0, 2, 3, 4, 5).reshape(B, S, Hq, D)


def _dilated_window_partial(q, k, v, window, dilation):
    B, S, H, D = q.shape
    span = (window // 2) // dilation
    L = S // dilation
    blk = span
    nb = -(-L // blk)
    Lp = nb * blk
    scale = D ** -0.5

    def strided(x):
        return x.reshape(B, L, dilation, H, D).transpose(0, 2, 1, 3, 4)

    qs = jnp.pad(strided(q), ((0, 0), (0, 0), (0, Lp - L), (0, 0), (0, 0)))
    kv_pad = ((0, 0), (0, 0), (blk, Lp - L + blk), (0, 0), (0, 0))
    ks = jnp.pad(strided(k), kv_pad).reshape(B, dilation, nb + 2, blk, H, D)
    vs = jnp.pad(strided(v), kv_pad).reshape(B, dilation, nb + 2, blk, H, D)
    qb = qs.reshape(B, dilation, nb, blk, H, D)
    kw = jnp.concatenate([ks[:, :, :-2], ks[:, :, 1:-1], ks[:, :, 2:]], axis=3)
    vw = jnp.concatenate([vs[:, :, :-2], vs[:, :, 1:-1], vs[:, :, 2:]], axis=3)

    qpos = jnp.arange(nb)[:, None] * blk + jnp.arange(blk)[None, :]
    kpos = jnp.arange(nb)[:, None] * blk + jnp.arange(3 * blk)[None, :] - blk
    diff = kpos[:, None, :] - qpos[:, :, None]
    valid = (jnp.abs(diff) <= span) & (kpos[:, None, :] >= 0) & (kpos[:, None, :] < L)

    s = jnp.einsum('brcihe,brcjhe->brchij', qb, kw, preferred_element_type=jnp.float32) * scale
    s = jnp.where(valid[None, None, :, None], s, NEG_INF)
    m = jnp.max(s, axis=-1)
    p = jnp.exp(s - m[..., None])
    l = jnp.sum(p, axis=-1)
    o = jnp.einsum('brchij,brcjhe->brcihe', p.astype(vw.dtype), vw, preferred_element_type=jnp.float32)

    o = o.reshape(B, dilation, Lp, H, D)[:, :, :L].transpose(0, 2, 1, 3, 4).reshape(B, S, H, D)

    def unstride_stat(a):
        a = a.transpose(0, 1, 2, 4, 3).reshape(B, dilation, Lp, H)[:, :, :L]
        return a.transpose(0, 2, 1, 3).reshape(B, S, H)

    return o, unstride_stat(m), unstride_stat(l)


def _dilated_mixture(q, k, v):
    parts = [_dilated_window_partial(q, k, v, w, d) for (w, d) in DILATED_CONFIGS]
    m_all = jnp.max(jnp.stack([p[1] for p in parts], axis=0), axis=0)
    num = jnp.zeros(q.shape, jnp.float32)
    den = jnp.zeros(q.shape[:3], jnp.float32)
    for o_i, m_i, l_i in parts:
        w_i = jnp.exp(m_i - m_all)
        num = num + w_i[..., None] * o_i
        den = den + w_i * l_i
    return (num / den[..., None]).astype(q.dtype)


def setup_inputs(seed: int = 0) -> dict:
    key = jax.random.key(seed)
    ks = jax.random.split(key, 18)
    f32 = jnp.float32

    def w(k, shape, fan_in):
        return jax.random.normal(k, shape, f32) * (fan_in ** -0.5)

    def gain(k, dim):
        return 1.0 + 0.02 * jax.random.normal(k, (DEPTH, dim), f32)

    return {
        "x": jax.random.normal(ks[0], (BATCH, SEQ, D_MODEL), f32),
        "ffn1_pre_g": gain(ks[1], D_MODEL),
        "ffn1_post_g": gain(ks[2], D_MODEL),
        "ffn1_w_gate": w(ks[3], (DEPTH, D_MODEL, D_FF), D_MODEL),
        "ffn1_w_up": w(ks[4], (DEPTH, D_MODEL, D_FF), D_MODEL),
        "ffn1_w_down": w(ks[5], (DEPTH, D_FF, D_MODEL), D_FF),
        "mix_pre_g": gain(ks[6], D_MODEL),
        "mix_post_g": gain(ks[7], D_MODEL),
        "w_qkv": w(ks[8], (DEPTH, D_MODEL, QKV_COLS), D_MODEL),
        "q_norm_g": gain(ks[9], HEAD_DIM),
        "k_norm_g": gain(ks[10], HEAD_DIM),
        "w_out": w(ks[11], (DEPTH, D_MIX, D_MODEL), D_MIX),
        "ffn2_pre_g": gain(ks[12], D_MODEL),
        "ffn2_post_g": gain(ks[13], D_MODEL),
        "ffn2_w_gate": w(ks[14], (DEPTH, D_MODEL, D_FF), D_MODEL),
        "ffn2_w_up": w(ks[15], (DEPTH, D_MODEL, D_FF), D_MODEL),
        "ffn2_w_down": w(ks[16], (DEPTH, D_FF, D_MODEL), D_FF),
    }


def reference(x, ffn1_pre_g, ffn1_post_g, ffn1_w_gate, ffn1_w_up, ffn1_w_down,
              mix_pre_g, mix_post_g, w_qkv, q_norm_g, k_norm_g, w_out,
              ffn2_pre_g, ffn2_post_g, ffn2_w_gate, ffn2_w_up, ffn2_w_down):
    B, S, _ = x.shape
    ROWS = S // GRID_W
    pos = jnp.arange(S, dtype=jnp.int32)
    row = jnp.repeat(jnp.arange(ROWS, dtype=jnp.int32), GRID_W)
    col = jnp.tile(jnp.arange(GRID_W, dtype=jnp.int32), ROWS)
    ang_row = _rope_angles(row, HEAD_DIM // 2)
    ang_col = _rope_angles(col, HEAD_DIM // 2)
    ang_1d = _rope_angles(pos, HEAD_DIM)

    for l in range(DEPTH):
        f = _swiglu(_rmsnorm(x, ffn1_pre_g[l]), ffn1_w_gate[l], ffn1_w_up[l], ffn1_w_down[l])
        x = x + 0.5 * _rmsnorm(f, ffn1_post_g[l])

        h = _rmsnorm(x, mix_pre_g[l])
        qkv = h @ w_qkv[l]
        qa, ka, va, qb, kb, vb = jnp.split(qkv, QKV_SPLITS, axis=-1)

        qa = _rmsnorm(qa.reshape(B, S, N_HEADS_A, HEAD_DIM), q_norm_g[l])
        ka = _rmsnorm(ka.reshape(B, S, N_KV_A, HEAD_DIM), k_norm_g[l])
        va = va.reshape(B, S, N_KV_A, HEAD_DIM)
        qa = _apply_axial_rope(qa, ang_row, ang_col)
        ka = _apply_axial_rope(ka, ang_row, ang_col)
        out_a = _gqa_dense_blocked(qa, ka, va)

        qb = _apply_rope(qb.reshape(B, S, N_HEADS_B, HEAD_DIM), ang_1d)
        kb = _apply_rope(kb.reshape(B, S, N_HEADS_B, HEAD_DIM), ang_1d)
        vb = vb.reshape(B, S, N_HEADS_B, HEAD_DIM)
        out_b = _dilated_mixture(qb, kb, vb)

        heads = jnp.concatenate([out_a.reshape(B, S, A_Q), out_b.reshape(B, S, B_QKV)], axis=-1)
        x = x + _rmsnorm(heads @ w_out[l], mix_post_g[l])

        f = _swiglu(_rmsnorm(x, ffn2_pre_g[l]), ffn2_w_gate[l], ffn2_w_up[l], ffn2_w_down[l])
        x = x + 0.5 * _rmsnorm(f, ffn2_post_g[l])
    return x
```

```python
import contextlib
import numpy as np
import ml_dtypes
import concourse.bass as bass
import concourse.mybir as mybir
from concourse.bass_utils import run_bass_kernel_spmd

F32 = mybir.dt.float32
BF16 = mybir.dt.bfloat16
AF = mybir.ActivationFunctionType
ALU = mybir.AluOpType
AX = mybir.AxisListType

NCORES = 8
T = 2048
NT = 16
D = 1024
KC = 8
FF = 2816
FC = 22
SEQ = 8192
EPS = 1e-6
SCALE = 0.125
NMASK = 8


class Trk:
    def __init__(self, nc, esems, dsems):
        self.nc = nc
        self.eng = {"pe": nc.tensor, "act": nc.scalar, "dve": nc.vector,
                    "pool": nc.gpsimd, "sp": nc.sync}
        self.sem = esems
        self.cnt = {k: 0 for k in esems}
        self.waited = {e: {} for e in self.eng}
        self.last_w = {}
        self.readers = {}
        self.dsems = dsems
        self.dcnt = [0] * len(dsems)
        self.dpool = {"sp": (0, 20), "act": (20, 28), "pool": (28, len(dsems))}
        self.drr = {"sp": 0, "act": 0, "pool": 0}

    def _wait(self, e, tok):
        if tok is None:
            return
        sem, val, src = tok
        if src == e and e == "pe":
            return
        k = id(sem)
        if self.waited[e].get(k, 0) >= val:
            return
        self.eng[e].wait_ge(sem, val)
        self.waited[e][k] = val

    def _deps(self, e, reads, writes):
        for b in reads:
            self._wait(e, self.last_w.get(b))
        for b in writes:
            self._wait(e, self.last_w.get(b))
            for t in self.readers.get(b, {}).values():
                self._wait(e, t)

    def _commit(self, tok, reads, writes):
        k = id(tok[0])
        for b in reads:
            self.readers.setdefault(b, {})[k] = tok
        for b in writes:
            self.last_w[b] = tok
            self.readers[b] = {}

    def op(self, e, fns, reads=(), writes=()):
        self._deps(e, reads, writes)
        if not isinstance(fns, (list, tuple)):
            fns = [fns]
        ins = None
        for f in fns:
            ins = f()
        self.cnt[e] += 1
        ins.then_inc(self.sem[e], 1)
        tok = (self.sem[e], self.cnt[e], e)
        self._commit(tok, reads, writes)
        return tok

    def dma(self, q, out, in_, reads=(), writes=()):
        self._deps(q, reads, writes)
        lo, hi = self.dpool[q]
        i = lo + self.drr[q]
        self.drr[q] = (self.drr[q] + 1) % (hi - lo)
        sem = self.dsems[i]
        if self.dcnt[i] > 0:
            self._wait(q, (sem, self.dcnt[i], "dma"))
        self.dcnt[i] += 16
        self.eng[q].dma_start(out=out, in_=in_).then_inc(sem, 16)
        tok = (sem, self.dcnt[i], "dma")
        self._commit(tok, reads, writes)
        return tok

    def barrier(self):
        for e in self.eng:
            for k, sem in self.sem.items():
                if self.cnt[k] > 0:
                    self._wait(e, (sem, self.cnt[k], "bar"))
            for i, sem in enumerate(self.dsems):
                if self.dcnt[i] > 0:
                    self._wait(e, (sem, self.dcnt[i], "dma"))

    def cc(self, sem, ins, outs, groups, reads=(), writes=()):
        e = "pool"
        self._deps(e, reads, writes)
        self.nc.gpsimd.collective_compute(
            "AllGather", ALU.bypass, replica_groups=groups, ins=ins, outs=outs
        ).then_inc(sem)
        tok = (sem, 1, "cc")
        self._commit(tok, reads, writes)
        return tok


def build_nc(debug=False, phases=("ffn1", "qkv", "attn", "outproj", "ffn2")):
    nc = bass.Bass("TRN2", target_bir_lowering=False)

    def din(name, shape, dt=F32):
        return nc.dram_tensor(name, list(shape), dt, kind="ExternalInput").ap()

    x_d = din("x", [T, D])
    wg_d = [din("wg1", [D, FF]), din("wg2", [D, FF])]
    wu_d = [din("wu1", [D, FF]), din("wu2", [D, FF])]
    wd_d = [din("wd1", [FF, D]), din("wd2", [FF, D])]
    wqkv_d = din("wqkv", [D, 2304])
    wout_d = din("wout", [D, D])
    gains_d = din("gains", [6, D])
    gqk_d = din("gqk", [2, 64])
    rope_d = din("rope", [T, 128])
    mask_d = din("maskb", [128, NMASK, 512], BF16)
    mk2_d = din("mk2", [128, 2, 128], BF16)
    ident_d = din("ident", [128, 128], BF16)
    out_d = nc.dram_tensor("out", [T, D], F32, kind="ExternalOutput").ap()

    x1_d = nc.dram_tensor("x1_park", [T, D], F32).ap()
    x2_d = nc.dram_tensor("x2_park", [T, D], F32).ap()
    xin_ka = nc.dram_tensor("xin_ka", [128, T], BF16)
    xin_va = nc.dram_tensor("xin_va", [T, 192], BF16)
    xin_kb = [nc.dram_tensor("xin_kb%d" % h, [256, T], BF16) for h in range(2)]
    xin_vb = [nc.dram_tensor("xin_vb%d" % p, [T, 192], BF16) for p in range(4)]
    g_ka = nc.dram_tensor("g_ka", [4 * 128, T], BF16)
    g_va = nc.dram_tensor("g_va", [4 * T, 192], BF16)
    big_kb = [nc.dram_tensor("big_kb%d" % h, [6 * 256, T], BF16) for h in range(2)]
    big_vb = [nc.dram_tensor("big_vb%d" % p, [6 * T, 192], BF16) for p in range(4)]
    win_kb = nc.dram_tensor("win_kb", [512, 4096], BF16)
    win_vb = nc.dram_tensor("win_vb", [4096, 768], BF16)

    groups = [[0, 1, 2, 3], [4, 5, 6, 7]]

    with contextlib.ExitStack() as es:
        def sb(name, shape, dt):
            return es.enter_context(nc.sbuf_tensor("s_" + name, list(shape), dt))

        esems = {k: es.enter_context(nc.semaphore("e_" + k)) for k in ("pe", "act", "dve", "pool")}
        dsems = [es.enter_context(nc.semaphore("d%d" % i)) for i in range(40)]
        ccs = [es.enter_context(nc.semaphore("cc%d" % i)) for i in range(8)]
        tk = Trk(nc, esems, dsems)
        pp = [es.enter_context(nc.psum_tensor("pp%d" % i, [128, 1024], F32)) for i in range(4)]

        ident = sb("ident", [128, 128], BF16)
        negh = sb("negh", [128, 8], F32)
        ones_bf = sb("ones_bf", [128, 64], BF16)
        zeros = sb("zeros", [128, 1024], BF16)
        tk.dma("sp", ident[:], ident_d, writes=["ident"])
        tk.op("pool", lambda: nc.gpsimd.memset(negh[:], -0.5), writes=["negh"])
        tk.op("pool", lambda: nc.gpsimd.memset(ones_bf[:], 1.0), writes=["ones"])
        tk.op("pool", lambda: nc.gpsimd.memset(zeros[:], 0.0), writes=["zeros"])
        def emit_pads():
            for slot, c0, r0 in ((0, 1024, 1024), (5, 0, 0)):
                for h in range(2):
                    tk.dma("act", big_kb[h][slot * 256:(slot + 1) * 256, c0:c0 + 1024].rearrange("(n p) t -> p n t", p=128),
                           zeros[:, 0:1024].unsqueeze(1).broadcast_to([128, 2, 1024]), reads=["zeros"], writes=[("big_kb_pad", h, slot)])
                for p in range(4):
                    tk.dma("act", big_vb[p][slot * T + r0:slot * T + r0 + 1024, :].rearrange("(n p) c -> p n c", p=128),
                           zeros[:, 0:192].unsqueeze(1).broadcast_to([128, 8, 192]), reads=["zeros"], writes=[("big_vb_pad", p, slot)])


        def rstd_pool(ss_ap, v_ap, r_ap, n, inv_n, key_ss, key_r, extra_scale=None):
            tk.op("pool", lambda: nc.gpsimd.tensor_scalar(out=v_ap, in0=ss_ap, scalar1=inv_n, scalar2=EPS,
                                                          op0=ALU.mult, op1=ALU.add),
                  reads=[key_ss], writes=[key_r + "_v"])
            tk.op("pool", lambda: nc.gpsimd.tensor_tensor(out=r_ap, in0=v_ap, in1=negh[:, 0:n], op=ALU.pow),
                  reads=[key_r + "_v", "negh"], writes=[key_r])

        def load_gain(dst, idx, key, half=False):
            tk.dma("sp", dst[:], gains_d[idx:idx + 1, :].broadcast_to([128, D]), writes=[key])
            if half:
                tk.op("dve", lambda: nc.vector.tensor_scalar(out=dst[:], in0=dst[:], scalar1=0.5, scalar2=None,
                                                             op0=ALU.mult), reads=[key], writes=[key])

        cgrp = [(0, 2), (2, 6), (6, 14), (14, 22)]

        def ffn_weights(pi, wg, wu, wd):
            P = "f%d_" % pi
            wgv = wg_d[pi].rearrange("(k p) f -> p k f", p=128)
            wuv = wu_d[pi].rearrange("(k p) f -> p k f", p=128)
            wdv = wd_d[pi].rearrange("(c p) n -> p c n", p=128)
            for g, (lo, hi) in enumerate(cgrp):
                fs_ = slice(lo * 128, hi * 128)
                tk.dma("pool", wg[:, :, fs_], wgv[:, :, fs_], writes=[P + "wg%d" % g])
                tk.dma("pool", wu[:, :, fs_], wuv[:, :, fs_], writes=[P + "wu%d" % g])
                tk.dma("pool", wd[:, lo:hi, :], wdv[:, lo:hi, :], writes=[P + "wd%d" % g])

        def ffn_phase(pi, src_d, src_key, dst_d, dst_key, gpre_i, gpost_i, pre=None):
            with contextlib.ExitStack() as fs:
                def fsb(name, shape, dt):
                    return fs.enter_context(nc.sbuf_tensor("f%d_%s" % (pi, name), list(shape), dt))
                if pre is None:
                    wg = fsb("wg", [128, KC, FF], BF16)
                    wu = fsb("wu", [128, KC, FF], BF16)
                    wd = fsb("wd", [128, FC, D], BF16)
                else:
                    wg, wu, wd, af = pre
                gpre = fsb("gpre", [128, D], F32)
                gpost = fsb("gpost", [128, D], F32)
                if pre is None:
                    xt = [fsb("xt%d" % i, [128, D], F32) for i in range(4)]
                    scr = fsb("scr", [128, D], F32)
                else:
                    xt = [af[:, i, :] for i in range(4)]
                    scr = af[:, 6, :]
                    dl = [af[:, 7, :], fsb("dl1", [128, D], F32)]
                xn = [fsb("xn%d" % i, [128, D], BF16) for i in range(2)]
                hT = [fsb("hT%d" % i, [128, KC, 256], BF16) for i in range(2)]
                sg = [fsb("sg%d" % i, [128, 256], F32) for i in range(2)]
                aT = [fsb("aT%d" % i, [128, 256], BF16) for i in range(4)]
                if pre is None:
                    ysb = [fsb("ysb%d" % i, [128, D], F32) for i in range(2)]
                else:
                    ysb = [af[:, 4 + i, :] for i in range(2)]
                st = fsb("st", [128, 16], F32)

                P = "f%d_" % pi
                load_gain(gpre, gpre_i, P + "gpre")
                load_gain(gpost, gpost_i, P + "gpost", half=True)
                if pre is None:
                    ffn_weights(pi, wg, wu, wd)
                grp_of = {}
                for g, (lo, hi) in enumerate(cgrp):
                    for c in range(lo, hi):
                        grp_of[c] = g

                pst = pp[3][:, 0:512].bitcast(BF16)

                def prepA0(blk, t2):
                    tt = blk * 2 + t2
                    xb = tt % 4
                    kx = P + "xt%d" % xb
                    tk.dma("sp", xt[xb][:], src_d[tt * 128:(tt + 1) * 128, :],
                           reads=[(src_key, tt)], writes=[kx])
                    if pre is not None:
                        tk.dma("sp", dl[t2][:], x1_d[tt * 128:(tt + 1) * 128, :], reads=[("x1", tt)], writes=[P + "dl%d" % t2])

                def prepA1(blk, t2):
                    tt = blk * 2 + t2
                    xb = tt % 4
                    kx = P + "xt%d" % xb
                    c0 = 3 * t2
                    if pre is not None:
                        tk.op("dve", lambda: nc.vector.tensor_tensor(out=xt[xb][:], in0=xt[xb][:], in1=dl[t2][:], op=ALU.add),
                              reads=[kx, P + "dl%d" % t2], writes=[kx])
                    tk.op("dve", lambda: nc.vector.tensor_tensor(out=scr[:], in0=xt[xb][:], in1=xt[xb][:], op=ALU.mult),
                          reads=[kx], writes=[P + "scr"])
                    tk.op("dve", lambda: nc.vector.tensor_reduce(out=st[:, c0:c0 + 1], in_=scr[:], axis=AX.X, op=ALU.add),
                          reads=[P + "scr"], writes=[P + "ss%d" % t2])
                    rstd_pool(st[:, c0:c0 + 1], st[:, c0 + 1:c0 + 2], st[:, c0 + 2:c0 + 3], 1, 1.0 / D, P + "ss%d" % t2, P + "r%d" % t2)

                def prepA2(blk, t2):
                    tt = blk * 2 + t2
                    xb = tt % 4
                    kx = P + "xt%d" % xb
                    c0 = 3 * t2
                    xnb = tt % 2
                    tk.op("dve", lambda: nc.vector.scalar_tensor_tensor(out=xn[xnb][:], in0=xt[xb][:], scalar=st[:, c0 + 2:c0 + 3],
                                                                       in1=gpre[:], op0=ALU.mult, op1=ALU.mult),
                          reads=[kx, P + "r%d" % t2, P + "gpre"], writes=[P + "xn%d" % xnb])

                def prepB(blk):
                    hb = blk % 2
                    for t2 in range(2):
                        tt = blk * 2 + t2
                        xnb = tt % 2
                        tk.op("pe", [(lambda k=k: nc.tensor.transpose(pst[:, k * 128:(k + 1) * 128],
                                                                     xn[xnb][:, k * 128:(k + 1) * 128], ident[:]))
                                     for k in range(KC)],
                              reads=[P + "xn%d" % xnb, "ident"], writes=["pp3a"])
                        tk.op("act", lambda: nc.scalar.activation(out=hT[hb][:, :, t2 * 128:(t2 + 1) * 128],
                                                                   in_=pst.rearrange("p (k t) -> p k t", k=KC),
                                                                   func=AF.Copy),
                              reads=["pp3a"], writes=[P + "hT%d_%d" % (hb, t2)])

                def down(blk, fc):
                    ab = fc % 4
                    fns = []
                    for t2 in range(2):
                        for hf in range(2):
                            fns.append(lambda t2=t2, hf=hf: nc.tensor.matmul(
                                pp[t2][:, hf * 512:(hf + 1) * 512],
                                lhsT=aT[ab][:, t2 * 128:(t2 + 1) * 128],
                                rhs=wd[:, fc, hf * 512:(hf + 1) * 512],
                                start=(fc == 0), stop=(fc == FC - 1)))
                    tk.op("pe", fns, reads=[P + "aT%d" % ab, P + "wd%d" % grp_of[fc]], writes=["pp0", "pp1"])

                def post0(blk):
                    for t2 in range(2):
                        tt = blk * 2 + t2
                        yb = tt % 2
                        tk.op("act", lambda: nc.scalar.activation(out=ysb[yb][:], in_=pp[t2][:], func=AF.Copy),
                              reads=["pp%d" % t2], writes=[P + "ysb%d" % yb])

                def postA(blk, t2):
                    tt = blk * 2 + t2
                    yb = tt % 2
                    ky = P + "ysb%d" % yb
                    c0 = 6 + 3 * t2
                    tk.op("dve", lambda: nc.vector.tensor_tensor(out=scr[:], in0=ysb[yb][:], in1=ysb[yb][:], op=ALU.mult),
                          reads=[ky], writes=[P + "scr"])
                    tk.op("dve", lambda: nc.vector.tensor_reduce(out=st[:, c0:c0 + 1], in_=scr[:], axis=AX.X, op=ALU.add),
                          reads=[P + "scr"], writes=[P + "pss%d" % t2])
                    rstd_pool(st[:, c0:c0 + 1], st[:, c0 + 1:c0 + 2], st[:, c0 + 2:c0 + 3], 1, 1.0 / D, P + "pss%d" % t2, P + "pr%d" % t2)

                def postB(blk, t2):
                    tt = blk * 2 + t2
                    xb = tt % 4
                    yb = tt % 2
                    ky = P + "ysb%d" % yb
                    c0 = 6 + 3 * t2
                    tk.op("dve", lambda: nc.vector.scalar_tensor_tensor(out=ysb[yb][:], in0=ysb[yb][:], scalar=st[:, c0 + 2:c0 + 3],
                                                                       in1=gpost[:], op0=ALU.mult, op1=ALU.mult),
                          reads=[ky, P + "pr%d" % t2, P + "gpost"], writes=[ky])
                    tk.op("dve", lambda: nc.vector.tensor_tensor(out=ysb[yb][:], in0=ysb[yb][:], in1=xt[xb][:], op=ALU.add),
                          reads=[ky, P + "xt%d" % xb], writes=[ky])
                    tk.dma("sp", dst_d[tt * 128:(tt + 1) * 128, :], ysb[yb][:], reads=[ky], writes=[(dst_key, tt)])

                nblk = NT // 2
                for t2 in range(2):
                    prepA0(0, t2)
                for t2 in range(2):
                    prepA1(0, t2)
                for t2 in range(2):
                    prepA2(0, t2)
                prepB(0)
                for blk in range(nblk):
                    hb = blk % 2
                    for fc in range(FC):
                        s = fc % 2
                        g = grp_of[fc]
                        fsl = slice(fc * 128, (fc + 1) * 128)
                        gu = pp[2][:, s * 512:(s + 1) * 512]
                        fns = []
                        for k in range(KC):
                            fns.append(lambda k=k: nc.tensor.matmul(gu[:, 0:256], lhsT=wg[:, k, fsl], rhs=hT[hb][:, k, :],
                                                                    start=(k == 0), stop=(k == KC - 1)))
                        for k in range(KC):
                            fns.append(lambda k=k: nc.tensor.matmul(gu[:, 256:512], lhsT=wu[:, k, fsl], rhs=hT[hb][:, k, :],
                                                                    start=(k == 0), stop=(k == KC - 1)))
                        tk.op("pe", fns, reads=[P + "wg%d" % g, P + "wu%d" % g, P + "hT%d_0" % hb, P + "hT%d_1" % hb],
                              writes=["pp2_%d" % s])
                        tk.op("act", lambda: nc.scalar.activation(out=sg[s][:], in_=gu[:, 0:256], func=AF.Silu),
                              reads=["pp2_%d" % s], writes=[P + "sg%d" % s])
                        ab = fc % 4
                        tk.op("dve", lambda: nc.vector.tensor_tensor(out=aT[ab][:], in0=sg[s][:], in1=gu[:, 256:512], op=ALU.mult),
                              reads=[P + "sg%d" % s, "pp2_%d" % s], writes=[P + "aT%d" % ab])
                        if fc >= 1:
                            down(blk, fc - 1)
                        if blk >= 1:
                            if fc in (1, 2):
                                postA(blk - 1, fc - 1)
                            if fc in (3, 4):
                                postB(blk - 1, fc - 3)
                        if blk + 1 < nblk:
                            if fc == 5:
                                prepA0(blk + 1, 0)
                                prepA0(blk + 1, 1)
                            if fc in (7, 8):
                                prepA1(blk + 1, fc - 7)
                            if fc in (9, 10):
                                prepA2(blk + 1, fc - 9)
                            if fc == 13:
                                prepB(blk + 1)
                        if pi == 0 and blk == 4 and fc == 16:
                            emit_pads()
                    down(blk, FC - 1)
                    post0(blk)
                for t2 in range(2):
                    postA(nblk - 1, t2)
                for t2 in range(2):
                    postB(nblk - 1, t2)

        mid = {}

        def qkv_phase():
            qaT, qbT, attnT = mid["qaT"], mid["qbT"], mid["attnT"]
            kaT_all, va_all = mid["kaT"], mid["va"]
            with contextlib.ExitStack() as fs:
                def fsb(name, shape, dt):
                    return fs.enter_context(nc.sbuf_tensor("q_" + name, list(shape), dt))
                wq = fsb("wq", [128, KC, 2304], BF16)
                gpre = fsb("gpre", [128, D], F32)
                gq = fsb("gq", [128, 2, 64], F32)
                rope = fsb("rope", [128, NT, 128], F32)
                xt = [fsb("xt%d" % i, [128, D], F32) for i in range(2)]
                xn = [fsb("xn%d" % i, [128, D], BF16) for i in range(2)]
                scr = fsb("scr", [128, D], F32)
                hT = mid["attnT"]
                st = fsb("st", [128, 64], F32)
                stg = [fsb("stg%d" % i, [128, 1536], F32) for i in range(2)]
                kvs = [fsb("kvs%d" % i, [128, 128], F32) for i in range(3)]
                t1 = fsb("t1", [128, 512], F32)
                t2b = fsb("t2", [128, 512], F32)
                t3 = fsb("t3", [128, 512], F32)
                t4 = fsb("t4", [128, 512], F32)
                t1p = t2p = t3p = t4p = None
                qr = [fsb("qr%d" % i, [128, 1536], BF16) for i in range(2)]
                kr = [fsb("kr%d" % i, [128, 128], BF16) for i in range(2)]
                kas = [fsb("kas%d" % i, [128, 128], BF16) for i in range(2)]
                kbs = [fsb("kbs%d" % i, [128, 4, 128], BF16) for i in range(2)]
                vax = [fsb("vax%d" % i, [128, 192], BF16) for i in range(2)]
                vbx = [fsb("vbx%d" % i, [128, 768], BF16) for i in range(2)]

                load_gain(gpre, 2, "q_gpre")
                tk.dma("sp", gq[:].rearrange("p a d -> p (a d)"),
                       gqk_d.rearrange("a d -> (a d)").unsqueeze(0).broadcast_to([128, 128]), writes=["q_gq"])
                tk.dma("sp", rope[:], rope_d.rearrange("(n p) c -> p n c", p=128), writes=["q_rope"])
                wqv = wqkv_d.rearrange("(k p) f -> p k f", p=128)
                tk.dma("pool", wq[:, :, 512:768], wqv[:, :, 512:768], writes=["q_wq_kv"])
                tk.dma("pool", wq[:, :, 0:512], wqv[:, :, 0:512], writes=["q_wq_qa"])
                for g in range(3):
                    tk.dma("pool", wq[:, :, 768 + g * 512:768 + (g + 1) * 512], wqv[:, :, 768 + g * 512:768 + (g + 1) * 512],
                           writes=["q_wq_b%d" % g])
                for i in range(2):
                    tk.op("pool", lambda i=i: nc.gpsimd.memset(vax[i][:], 1.0), writes=["q_vax%d" % i])
                    tk.op("pool", lambda i=i: nc.gpsimd.memset(vbx[i][:], 1.0), writes=["q_vbx%d" % i])

                def rope_ops(e, src, src_keys, nh, cos, sin, axial, dst, dst_key):
                    E = nc.vector if e == "dve" else nc.gpsimd
                    scrs = (t1, t2b, t3, t4) if e == "dve" else (t1p, t2p, t3p, t4p)
                    tkey = "q_t" if e == "dve" else "q_tp"
                    if axial:
                        sv = src.rearrange("p (h a b d) -> p h a b d", h=nh, a=2, b=2, d=16)
                        dv = dst.rearrange("p (h a b d) -> p h a b d", h=nh, a=2, b=2, d=16)
                        x1, x2 = sv[:, :, :, 0, :], sv[:, :, :, 1, :]
                        d1, d2 = dv[:, :, :, 0, :], dv[:, :, :, 1, :]
                        cb = cos.rearrange("p (a d) -> p a d", a=2).unsqueeze(1).broadcast_to([128, nh, 2, 16])
                        sn = sin.rearrange("p (a d) -> p a d", a=2).unsqueeze(1).broadcast_to([128, nh, 2, 16])
                        ta, tb, tc, td = [t_[:, 0:nh * 32].rearrange("p (h a d) -> p h a d", h=nh, a=2) for t_ in scrs]
                    else:
                        sv = src.rearrange("p (h b d) -> p h b d", h=nh, b=2, d=32)
                        dv = dst.rearrange("p (h b d) -> p h b d", h=nh, b=2, d=32)
                        x1, x2 = sv[:, :, 0, :], sv[:, :, 1, :]
                        d1, d2 = dv[:, :, 0, :], dv[:, :, 1, :]
                        cb = cos.unsqueeze(1).broadcast_to([128, nh, 32])
                        sn = sin.unsqueeze(1).broadcast_to([128, nh, 32])
                        ta, tb, tc, td = [t_[:, 0:nh * 32].rearrange("p (h d) -> p h d", h=nh) for t_ in scrs]
                    rk = list(src_keys) + ["q_rope"]
                    k1, k2, k3, k4 = [tkey + str(i_) for i_ in range(4)]
                    tk.op(e, lambda: E.tensor_tensor(out=ta, in0=x1, in1=cb, op=ALU.mult), reads=rk, writes=[k1])
                    tk.op(e, lambda: E.tensor_tensor(out=tb, in0=x2, in1=sn, op=ALU.mult), reads=rk, writes=[k2])
                    tk.op(e, lambda: E.tensor_tensor(out=tc, in0=x2, in1=cb, op=ALU.mult), reads=rk, writes=[k3])
                    tk.op(e, lambda: E.tensor_tensor(out=td, in0=x1, in1=sn, op=ALU.mult), reads=rk, writes=[k4])
                    tk.op(e, lambda: E.tensor_tensor(out=d1, in0=ta, in1=tb, op=ALU.subtract), reads=[k1, k2], writes=[dst_key])
                    tk.op(e, lambda: E.tensor_tensor(out=d2, in0=tc, in1=td, op=ALU.add), reads=[k3, k4], writes=[dst_key])

                def headnorm_a(dstv, key, nh, gi, stc):
                    n = nh * 64
                    sc_ = scr
                    sk_ = "q_scr"
                    tk.op("dve", lambda: nc.vector.tensor_tensor(out=sc_[:, 0:n], in0=dstv, in1=dstv, op=ALU.mult),
                          reads=[key], writes=[sk_])
                    tk.op("dve", lambda: nc.vector.tensor_reduce(out=st[:, stc:stc + nh],
                                                                 in_=sc_[:, 0:n].rearrange("p (h d) -> p h d", h=nh),
                                                                 axis=AX.X, op=ALU.add),
                          reads=[sk_], writes=["q_ssq%d" % stc])
                    rstd_pool(st[:, stc:stc + nh], st[:, stc + 8:stc + 8 + nh], st[:, stc + 16:stc + 16 + nh], nh, 1.0 / 64,
                              "q_ssq%d" % stc, "q_rq%d" % stc)

                def headnorm_b(dstv, key, nh, gi, stc):
                    d3 = dstv.rearrange("p (h d) -> p h d", h=nh)
                    tk.op("dve", lambda: nc.vector.tensor_tensor(
                        out=d3, in0=d3, in1=st[:, stc + 16:stc + 16 + nh].unsqueeze(2).broadcast_to([128, nh, 64]), op=ALU.mult),
                        reads=[key, "q_rq%d" % stc], writes=[key])
                    tk.op("dve", lambda: nc.vector.tensor_tensor(
                        out=d3, in0=d3, in1=gq[:, gi, :].unsqueeze(1).broadcast_to([128, nh, 64]), op=ALU.mult),
                        reads=[key, "q_gq"], writes=[key])

                pst = [pp[3][:, 0:512].bitcast(BF16), pp[3][:, 512:1024].bitcast(BF16)]
                pkv = [pp[0][:, 0:256], pp[0][:, 512:768]]
                pko = [pp[1][:, 0:512].bitcast(BF16), pp[1][:, 512:1024].bitcast(BF16)]

                def p1_load(tt):
                    xb = tt % 2
                    tk.dma("sp", xt[xb][:], x1_d[tt * 128:(tt + 1) * 128, :], reads=[("x1", tt)], writes=["q_xt%d" % xb])

                def p1_B(tt):
                    xb = tt % 2
                    c0 = 3 * xb
                    tk.op("dve", lambda: nc.vector.tensor_tensor(out=scr[:], in0=xt[xb][:], in1=xt[xb][:], op=ALU.mult),
                          reads=["q_xt%d" % xb], writes=["q_scr"])
                    tk.op("dve", lambda: nc.vector.tensor_reduce(out=st[:, c0:c0 + 1], in_=scr[:], axis=AX.X, op=ALU.add),
                          reads=["q_scr"], writes=["q_ss%d" % xb])
                    rstd_pool(st[:, c0:c0 + 1], st[:, c0 + 1:c0 + 2], st[:, c0 + 2:c0 + 3], 1, 1.0 / D, "q_ss%d" % xb, "q_r%d" % xb)

                def p1_E(tt):
                    xb = tt % 2
                    c0 = 3 * xb
                    tk.op("dve", lambda: nc.vector.scalar_tensor_tensor(out=xn[xb][:], in0=xt[xb][:], scalar=st[:, c0 + 2:c0 + 3],
                                                                       in1=gpre[:], op0=ALU.mult, op1=ALU.mult),
                          reads=["q_xt%d" % xb, "q_r%d" % xb, "q_gpre"], writes=["q_xn%d" % xb])
                    tk.op("pe", [(lambda k=k: nc.tensor.transpose(pst[xb][:, k * 128:(k + 1) * 128],
                                                                 xn[xb][:, k * 128:(k + 1) * 128], ident[:]))
                                 for k in range(KC)], reads=["q_xn%d" % xb, "ident"], writes=["q_pst%d" % xb])
                    tk.op("act", lambda: nc.scalar.activation(out=hT[:, :, tt * 128:(tt + 1) * 128],
                                                               in_=pst[xb].rearrange("p (k t) -> p k t", k=KC), func=AF.Copy),
                          reads=["q_pst%d" % xb], writes=[("q_hT", tt)])

                def p1_C(tt):
                    b = tt % 2
                    b3 = tt % 3
                    tk.op("pe", [(lambda k=k: nc.tensor.matmul(pkv[b], lhsT=hT[:, k, tt * 128:(tt + 1) * 128], rhs=wq[:, k, 512:768],
                                                               start=(k == 0), stop=(k == KC - 1))) for k in range(KC)],
                          reads=[("q_hT", tt), "q_wq_kv"], writes=["q_pkv%d" % b])
                    tk.op("act", lambda: nc.scalar.activation(out=kvs[b3][:], in_=pkv[b][:, 0:128], func=AF.Copy),
                          reads=["q_pkv%d" % b], writes=["q_kvs%d" % b3])
                    tk.op("act", lambda: nc.scalar.activation(
                        out=vax[b][:].rearrange("p (s d) -> p s d", s=3)[:, 0:3:2, :],
                        in_=pkv[b][:, 128:256].rearrange("p (s d) -> p s d", s=2), func=AF.Copy),
                        reads=["q_pkv%d" % b], writes=["q_vax%d" % b])
                    tk.dma("sp", xin_va[tt * 128:(tt + 1) * 128, :], vax[b][:], reads=["q_vax%d" % b], writes=[("xin_va", tt)])

                def p1_D(tt):
                    b3 = tt % 3
                    headnorm_a(kvs[b3][:], "q_kvs%d" % b3, 2, 1, 32 + 2 * b3)

                def p1_F(tt):
                    b = tt % 2
                    b3 = tt % 3
                    headnorm_b(kvs[b3][:], "q_kvs%d" % b3, 2, 1, 32 + 2 * b3)
                    rope_ops("dve", kvs[b3][:], ["q_kvs%d" % b3], 2, rope[:, tt, 0:32], rope[:, tt, 32:64], True,
                             kr[b][:], "q_kr%d" % b)
                    tk.op("pe", lambda: nc.tensor.transpose(pko[b][:, 0:128], kr[b][:], ident[:]),
                          reads=["q_kr%d" % b, "ident"], writes=["q_pko%d" % b])
                    tk.op("act", lambda: nc.scalar.activation(out=kas[b][:], in_=pko[b][:, 0:128], func=AF.Copy),
                          reads=["q_pko%d" % b], writes=["q_kas%d" % b])
                    tk.dma("sp", xin_ka[:, tt * 128:(tt + 1) * 128], kas[b][:], reads=["q_kas%d" % b], writes=[("xin_ka", tt)])

                p1_load(0)
                p1_load(1)
                p1_B(0)
                p1_E(0)
                for tt in range(NT):
                    if tt + 1 < NT:
                        p1_B(tt + 1)
                    p1_C(tt)
                    p1_D(tt)
                    if tt >= 1:
                        p1_F(tt - 1)
                    if tt + 1 < NT:
                        p1_E(tt + 1)
                    if tt + 2 < NT:
                        p1_load(tt + 2)
                p1_F(NT - 1)
                tk.cc(ccs[0], [xin_ka.ap().opt()], [g_ka.ap().opt()], groups, reads=[("xin_ka", t_) for t_ in range(NT)], writes=["g_ka"])
                tk.cc(ccs[1], [xin_va.ap().opt()], [g_va.ap().opt()], groups, reads=[("xin_va", t_) for t_ in range(NT)], writes=["g_va"])
                tk.dma("sp", kaT_all[:], g_ka.ap().rearrange("(r p) t -> p r t", p=128), reads=["g_ka"], writes=["a_kaT"])
                for h in range(4):
                    for h2 in range(2):
                        r0 = h * T + h2 * 1024
                        tk.dma("sp", va_all[:, h * 16 + h2 * 8:h * 16 + h2 * 8 + 8, :],
                               g_va[r0:r0 + 1024, :].rearrange("(n p) c -> p n c", p=128), reads=["g_va"],
                               writes=[("a_va", h * 2 + h2)])

                tk._wait("pe", (tk.sem["act"], tk.cnt["act"], "bar"))
                grp = [(pp[0][:, 0:512], slice(0, 512), "q_wq_qa", "q_pg0"),
                       (pp[0][:, 512:1024], slice(768, 1280), "q_wq_b0", "q_pg1"),
                       (pp[1][:, 0:512], slice(1280, 1792), "q_wq_b1", "q_pg2"),
                       (pp[1][:, 512:1024], slice(1792, 2304), "q_wq_b2", "q_pg3")]
                pso = pp[2][:].bitcast(BF16)
                def mm2(tt):
                    b = tt % 2
                    for gi, (pap, csl, wkey, pkey) in enumerate(grp):
                        tk.op("pe", [(lambda k=k, pap=pap, csl=csl: nc.tensor.matmul(pap, lhsT=hT[:, k, tt * 128:(tt + 1) * 128],
                                                                                   rhs=wq[:, k, csl], start=(k == 0), stop=(k == KC - 1)))
                                     for k in range(KC)],
                              reads=[("q_hT", tt), wkey], writes=[pkey])
                        if gi < 3:
                            tk.op("act", lambda gi=gi, pap=pap: nc.scalar.activation(out=stg[b][:, gi * 512:(gi + 1) * 512], in_=pap,
                                                                                   func=AF.Copy),
                                  reads=[pkey], writes=[("q_stg", b, gi)])
                        else:
                            tk.op("act", lambda: nc.scalar.activation(
                                out=vbx[b][:].rearrange("p (r s d) -> p r s d", r=4, s=3)[:, :, 0:3:2, :],
                                in_=pap.rearrange("p (r s d) -> p r s d", r=4, s=2), func=AF.Copy),
                                reads=[pkey], writes=["q_vbx%d" % b])
                            for p in range(4):
                                tk.dma("act", xin_vb[p][tt * 128:(tt + 1) * 128, :], vbx[b][:, p * 192:(p + 1) * 192],
                                       reads=["q_vbx%d" % b], writes=[("xin_vb", p, tt)])

                def chain2(tt):
                    b = tt % 2
                    cA, sA = rope[:, tt, 0:32], rope[:, tt, 32:64]
                    cB, sB = rope[:, tt, 64:96], rope[:, tt, 96:128]
                    headnorm_a(stg[b][:, 0:512], ("q_stg", b, 0), 8, 0, 8)
                    rope_ops("dve", stg[b][:, 512:1536], [("q_stg", b, 1), ("q_stg", b, 2)], 16, cB, sB, False,
                             qr[b][:, 512:1536], ("q_qr", b, 1))
                    headnorm_b(stg[b][:, 0:512], ("q_stg", b, 0), 8, 0, 8)
                    rope_ops("dve", stg[b][:, 0:512], [("q_stg", b, 0)], 8, cA, sA, True, qr[b][:, 0:512], ("q_qr", b, 0))
                    tk.op("pe", [(lambda i=i: nc.tensor.transpose(pso[:, i * 128:(i + 1) * 128],
                                                                 qr[b][:, i * 128:(i + 1) * 128], ident[:]))
                                 for i in range(12)],
                          reads=[("q_qr", b, 0), ("q_qr", b, 1), "ident"], writes=["q_pso"])
                    tk.op("act", lambda: nc.scalar.activation(out=qaT[:, tt, :, :],
                                                               in_=pso[:, 0:512].rearrange("p (g q) -> p g q", g=4), func=AF.Copy),
                          reads=["q_pso"], writes=["qaT"])
                    tk.op("act", lambda: nc.scalar.activation(out=qbT[:, :, tt * 128:(tt + 1) * 128],
                                                               in_=pso[:, 512:1024].rearrange("p (g q) -> p g q", g=4), func=AF.Copy),
                          reads=["q_pso"], writes=["qbT"])
                    tk.op("act", lambda: nc.scalar.activation(out=kbs[b][:],
                                                               in_=pso[:, 1024:1536].rearrange("p (g q) -> p g q", g=4), func=AF.Copy),
                          reads=["q_pso"], writes=["q_kbs%d" % b])
                    for h in range(2):
                        tk.dma("act", xin_kb[h].ap().rearrange("(g p) t -> p g t", p=128)[:, :, tt * 128:(tt + 1) * 128],
                               kbs[b][:, 2 * h:2 * h + 2, :], reads=["q_kbs%d" % b], writes=[("xin_kb", h, tt)])

                mm2(0)
                for tt in range(NT):
                    if tt + 1 < NT:
                        mm2(tt + 1)
                    chain2(tt)
                if "nocc" in phases:
                    return
                for h in range(2):
                    tk.cc(ccs[2 + h], [xin_kb[h].ap().opt()], [big_kb[h][256:5 * 256, :]], groups,
                          reads=[("xin_kb", h, t_) for t_ in range(NT)], writes=["big_kb%d" % h])
                for p in range(4):
                    tk.cc(ccs[4 + p], [xin_vb[p].ap().opt()], [big_vb[p][T:5 * T, :]], groups,
                          reads=[("xin_vb", p, t_) for t_ in range(NT)], writes=["big_vb%d" % p])

        def normalize1(po, okey, Osb):
            tk.op("dve", lambda: nc.vector.tensor_copy(out=Osb[:], in_=po[:]), reads=[okey], writes=["Osb"])

        def normalize2(dst0, dst1, R, Osb, Rl, use_act):
            g3 = len(dst0.shape) == 3
            def v(ap):
                return ap.rearrange("p (g q) -> p g q", g=dst0.shape[1]) if g3 else ap
            if use_act:
                tk.op("act", lambda: nc.scalar.activation(out=Rl[64:128, :], in_=Osb[64:128, 0:512], func=AF.Ln),
                      reads=["Osb"], writes=["Rl0"])
                tk.op("act", lambda: nc.scalar.activation(out=Rl[64:128, :], in_=Rl[64:128, :], func=AF.Exp, scale=-1.0),
                      reads=["Rl0"], writes=["Rl0"])
                tk.op("act", lambda: nc.scalar.activation(out=Rl[0:64, :], in_=Osb[0:64, 512:1024], func=AF.Ln),
                      reads=["Osb"], writes=["Rl1"])
                tk.op("act", lambda: nc.scalar.activation(out=Rl[0:64, :], in_=Rl[0:64, :], func=AF.Exp, scale=-1.0),
                      reads=["Rl1"], writes=["Rl1"])
                tk.op("dve", lambda: nc.vector.tensor_copy(out=R[0:64, :], in_=Rl[64:128, :]), reads=["Rl0"], writes=["R0"])
                tk.op("dve", lambda: nc.vector.tensor_copy(out=R[64:128, :], in_=Rl[0:64, :]), reads=["Rl1"], writes=["R1"])
            else:
                tk.op("dve", lambda: nc.vector.reciprocal(out=R[0:64, :], in_=Osb[64:128, 0:512]), reads=["Osb"], writes=["R0"])
                tk.op("dve", lambda: nc.vector.reciprocal(out=R[64:128, :], in_=Osb[0:64, 512:1024]), reads=["Osb"], writes=["R1"])
            tk.op("dve", lambda: nc.vector.tensor_tensor(out=dst0, in0=v(Osb[0:64, 0:512]), in1=v(R[0:64, :]), op=ALU.mult),
                  reads=["Osb", "R0"], writes=["attnT"])
            tk.op("dve", lambda: nc.vector.tensor_tensor(out=dst1, in0=v(Osb[64:128, 512:1024]), in1=v(R[64:128, :]), op=ALU.mult),
                  reads=["Osb", "R1"], writes=["attnT"])

        def attn_phase():
            qaT, qbT, attnT = mid["qaT"], mid["qbT"], mid["attnT"]
            kaT, va = mid["kaT"], mid["va"]
            NP = 5
            with contextlib.ExitStack() as fs:
                def fsb(name, shape, dt):
                    return fs.enter_context(nc.sbuf_tensor("a_" + name, list(shape), dt))
                Pb = [fsb("P%d" % i, [128, 1024], BF16) for i in range(NP)]
                R = fsb("R", [128, 512], F32)
                Osb = fsb("Osb", [128, 1024], F32)
                Rl = fsb("Rl", [128, 512], F32)
                maskb = fsb("maskb", [128, NMASK, 512], BF16)
                kbw = [fsb("kbw%d" % i, [128, 4096], BF16) for i in range(2)]
                vbw = [fsb("vbw%d" % i, [128, 20, 192], BF16) for i in range(2)]
                vbs = [fsb("vbs%d" % i, [128, 16, 2, 192], BF16) for i in range(2)]
                mk2 = fsb("mk2", [128, 2, 128], BF16)
                O2sb = fsb("O2sb", [128, 4, 1024], F32)

                tk.dma("sp", maskb[:], mask_d, writes=["a_mask"])
                tk.dma("sp", mk2[:], mk2_d, writes=["a_mk2"])
                va_keys = [("a_va", h) for h in range(8)]
                j = nc.sync.partition_id() % 4
                po = pp[3]

                steps = [(qb, kt) for qb in range(NT) for kt in range(64)]
                n = len(steps)

                def qk_a(i):
                    qb, kt = steps[i]
                    s = i % 2
                    ps = pp[s]
                    kr_, kc = kt // 16, (kt % 16) * 128
                    tk.op("pe", [lambda: nc.tensor.matmul(ps[:, 0:512], lhsT=kaT[0:64, kr_, kc:kc + 128],
                                                          rhs=qaT[0:64, qb, :, :].rearrange("p g q -> p (g q)"),
                                                          start=True, stop=True),
                                 lambda: nc.tensor.matmul(ps[:, 512:1024], lhsT=kaT[64:128, kr_, kc:kc + 128],
                                                          rhs=qaT[64:128, qb, :, :].rearrange("p g q -> p (g q)"),
                                                          start=True, stop=True)],
                          reads=["a_kaT", "qaT"], writes=["ppS%d" % s])
                    pb = i % NP
                    tk.op("act", lambda: nc.scalar.activation(out=Pb[pb][:], in_=ps[:], func=AF.Exp, scale=SCALE),
                          reads=["ppS%d" % s], writes=["a_P%d" % pb])

                def pv_a(i):
                    qb, kt = steps[i]
                    pb = i % NP
                    poA, keyA = (pp[2], "ppS2") if qb % 2 == 1 else (po, "ppO")
                    tk.op("pe", [lambda: nc.tensor.matmul(poA[:, 0:512], lhsT=va[:, kt, 0:128], rhs=Pb[pb][:, 0:512],
                                                          start=(kt == 0), stop=(kt == 63)),
                                 lambda: nc.tensor.matmul(poA[:, 512:1024], lhsT=va[:, kt, 64:192], rhs=Pb[pb][:, 512:1024],
                                                          start=(kt == 0), stop=(kt == 63))],
                          reads=["a_P%d" % pb] + va_keys, writes=[keyA])
                    if kt == 63:
                        normalize1(poA, keyA, Osb)
                        normalize2(attnT[0:64, 0:4, qb * 128:(qb + 1) * 128],
                                   attnT[64:128, 0:4, qb * 128:(qb + 1) * 128], R, Osb, Rl, False)

                qk_a(0)
                qk_a(1)
                for i in range(n):
                    if i + 2 < n:
                        qk_a(i + 2)
                    pv_a(i)

                for h in range(2):
                    for part, (slot_off, c0, c1) in enumerate([(0, 1024, 2048), (1, 0, 1024), (1, 1024, 2048), (2, 0, 1024)]):
                        tk.dma("sp", win_kb[h * 256:(h + 1) * 256, part * 1024:(part + 1) * 1024],
                               big_kb[h][bass.ds((j + slot_off) * 256, 256), c0:c1],
                               reads=["big_kb%d" % h, ("big_kb_pad", h, 0), ("big_kb_pad", h, 5)], writes=[("win_kb", h, part)])
                for p in range(4):
                    for qd in range(4):
                        tk.dma("sp", win_vb[qd * 1024:(qd + 1) * 1024, p * 192:(p + 1) * 192],
                               big_vb[p][bass.ds(j * T + 1024 + qd * 1024, 1024), :],
                               reads=["big_vb%d" % p, ("big_vb_pad", p, 0), ("big_vb_pad", p, 5)], writes=[("win_vb", p, qd)])

                def load_b(pair):
                    b = pair % 2
                    tk.dma("sp", kbw[b][:], win_kb[pair * 128:(pair + 1) * 128, :],
                           reads=[("win_kb", pair // 2, q_) for q_ in range(4)], writes=["a_kbw%d" % b])
                    for h in range(2):
                        r0 = 768 + h * 1280
                        tk.dma("sp", vbw[b][:, h * 10:(h + 1) * 10, :],
                               win_vb[r0:r0 + 1280, pair * 192:(pair + 1) * 192]
                               .rearrange("(n p) c -> p n c", p=128),
                               reads=[("win_vb", pair, q_) for q_ in range(4)], writes=[("a_vbw", b, h)])
                    wv = win_vb[:, pair * 192:(pair + 1) * 192].rearrange("(t p r) c -> p r t c", t=2, p=128, r=16)
                    for rg in range(4):
                        for t_ in range(2):
                            tk.dma("sp", vbs[b][:, rg * 4:(rg + 1) * 4, t_, :], wv[:, rg * 4:(rg + 1) * 4, t_, :],
                                   reads=[("win_vb", pair, q_) for q_ in range(4)], writes=[("a_vbs", b, rg, t_)])

                stepsb = []
                for pair in range(4):
                    stepsb += [("s2", pair, rg, t) for rg in range(4) for t in range(2)]
                    stepsb += [("s1", pair, qk, m) for qk in range(4) for m in range(NMASK)]
                nb_ = len(stepsb)

                def qk_b(i):
                    kind, pair, x1_, x2_ = stepsb[i]
                    b = pair % 2
                    s = i % 2
                    ps = pp[s]
                    pb = i % NP
                    if kind == "s1":
                        qk, m = x1_, x2_
                        wt = 6 + qk * 4 + m
                        fns = [lambda: nc.tensor.matmul(ps[:, 0:512], lhsT=kbw[b][0:64, wt * 128:(wt + 1) * 128],
                                                        rhs=qbT[0:64, pair, qk * 512:(qk + 1) * 512], start=True, stop=True),
                               lambda: nc.tensor.matmul(ps[:, 512:1024], lhsT=kbw[b][64:128, wt * 128:(wt + 1) * 128],
                                                        rhs=qbT[64:128, pair, qk * 512:(qk + 1) * 512], start=True, stop=True)]
                        mk = maskb[:, m, :].unsqueeze(1).broadcast_to([128, 2, 512])
                        pv_ = Pb[pb][:].rearrange("p (h q) -> p h q", h=2)
                        mkey = "a_mask"
                    else:
                        rg, t = x1_, x2_
                        fns = []
                        for rs in range(4):
                            r = rg * 4 + rs
                            for h in range(2):
                                c0 = (h * 4 + rs) * 128
                                k0 = 2048 * t + r
                                fns.append(lambda h=h, c0=c0, k0=k0, r=r: nc.tensor.matmul(
                                    ps[:, c0:c0 + 128], lhsT=kbw[b][h * 64:(h + 1) * 64, k0:k0 + 2033:16],
                                    rhs=qbT[h * 64:(h + 1) * 64, pair, r:2048:16], start=True, stop=True))
                        mk = mk2[:, t, :].unsqueeze(1).broadcast_to([128, 8, 128])
                        pv_ = Pb[pb][:].rearrange("p (h q) -> p h q", h=8)
                        mkey = "a_mk2"
                    tk.op("pe", fns, reads=["a_kbw%d" % b, "qbT"], writes=["ppS%d" % s])
                    tk.op("act", lambda: nc.scalar.activation(out=Pb[pb][:], in_=ps[:], func=AF.Exp, scale=SCALE),
                          reads=["ppS%d" % s], writes=["a_P%d" % pb])
                    tk.op("dve", lambda: nc.vector.tensor_tensor(out=pv_, in0=pv_, in1=mk, op=ALU.mult),
                          reads=["a_P%d" % pb, mkey], writes=["a_P%d" % pb])

                def pv_b(i):
                    kind, pair, x1_, x2_ = stepsb[i]
                    b = pair % 2
                    pb = i % NP
                    if kind == "s1":
                        qk, m = x1_, x2_
                        wt = 6 + qk * 4 + m
                        poX, okeyX = (pp[2], "ppS2") if qk % 2 == 1 else (po, "ppO")
                        tk.op("pe", [lambda: nc.tensor.matmul(poX[:, 0:512], lhsT=vbw[b][:, wt - 6, 0:128], rhs=Pb[pb][:, 0:512],
                                                              start=(m == 0), stop=(m == NMASK - 1)),
                                     lambda: nc.tensor.matmul(poX[:, 512:1024], lhsT=vbw[b][:, wt - 6, 64:192], rhs=Pb[pb][:, 512:1024],
                                                              start=(m == 0), stop=(m == NMASK - 1))],
                              reads=["a_P%d" % pb, ("a_vbw", b, 0), ("a_vbw", b, 1)], writes=[okeyX])
                        if m == NMASK - 1:
                            for h in range(2):
                                dstv = Osb[:, h * 512:(h + 1) * 512].rearrange("p (i g s) -> p i g s", i=32, g=4, s=4)
                                pov = poX[:, h * 512:(h + 1) * 512].rearrange("p (i g s) -> p i g s", i=32, g=4, s=4)
                                srcv = O2sb[:, :, h * 512:(h + 1) * 512].rearrange("p g (s i) -> p g s i", s=4)[
                                    :, :, :, 32 * qk:32 * qk + 32].rearrange("p g s i -> p i g s")
                                tk.op("dve", lambda dstv=dstv, pov=pov, srcv=srcv: nc.vector.tensor_tensor(out=dstv, in0=pov, in1=srcv, op=ALU.add),
                                      reads=[okeyX] + [("O2sb", g_) for g_ in range(4)], writes=["Osb"])
                            pend.append((i + 5, (attnT[0:64, 4 + pair, qk * 512:(qk + 1) * 512],
                                                 attnT[64:128, 4 + pair, qk * 512:(qk + 1) * 512])))
                    else:
                        rg, t = x1_, x2_
                        po2, okey2 = (pp[2], "ppS2") if rg % 2 == 1 else (po, "ppO")
                        fns = []
                        for rs in range(4):
                            r = rg * 4 + rs
                            for h in range(2):
                                c0 = (h * 4 + rs) * 128
                                fns.append(lambda h=h, c0=c0, r=r, rs=rs: nc.tensor.matmul(
                                    po2[:, c0:c0 + 128], lhsT=vbs[b][:, r, t, 64 * h:64 * h + 128], rhs=Pb[pb][:, c0:c0 + 128],
                                    start=(t == 0 and rs == 0), stop=(t == 1), skip_group_check=True))
                        tk.op("pe", fns, reads=["a_P%d" % pb, ("a_vbs", b, rg, t)], writes=[okey2])
                        if t == 1:
                            tk.op("dve", lambda: nc.vector.tensor_copy(out=O2sb[:, rg, :], in_=po2[:]),
                                  reads=[okey2], writes=[("O2sb", rg)])

                pend = []
                load_b(0)
                load_b(1)
                qk_b(0)
                qk_b(1)
                for i in range(nb_):
                    if pend and pend[0][0] <= i:
                        d0_, d1_ = pend.pop(0)[1]
                        normalize2(d0_, d1_, R, Osb, Rl, True)
                    if i + 2 < nb_:
                        qk_b(i + 2)
                    pv_b(i)
                    kind, pair = stepsb[i][0], stepsb[i][1]
                    last_of_pair = (i + 1 == nb_) or (stepsb[i + 1][1] != pair)
                    if last_of_pair and pair + 2 < 4:
                        load_b(pair + 2)
                while pend:
                    d0_, d1_ = pend.pop(0)[1]
                    normalize2(d0_, d1_, R, Osb, Rl, True)

        def outproj_phase(after_wo=None):
            qaT, qbT, attnT = mid["qaT"], mid["qbT"], mid["attnT"]
            with contextlib.ExitStack() as fs:
                def fsb(name, shape, dt):
                    return fs.enter_context(nc.sbuf_tensor("o_" + name, list(shape), dt))
                wo = fsb("wo", [128, KC, D], BF16)
                gpost = fsb("gpost", [128, D], F32)
                ysb = [fsb("ysb%d" % i, [128, D], F32) for i in range(3)]
                scr = fsb("scr", [128, D], F32)
                st = fsb("st", [128, 16], F32)
                load_gain(gpost, 3, "o_gpost")
                tk.dma("pool", wo[:], wout_d.rearrange("(k p) n -> p k n", p=128), writes=["o_wo"])
                if after_wo is not None:
                    after_wo()
                def o_front(tt):
                    b = tt % 2
                    b3 = tt % 3
                    fns = []
                    for hf in range(2):
                        for k in range(KC):
                            fns.append(lambda k=k, hf=hf: nc.tensor.matmul(
                                pp[b][:, hf * 512:(hf + 1) * 512], lhsT=attnT[:, k, tt * 128:(tt + 1) * 128],
                                rhs=wo[:, k, hf * 512:(hf + 1) * 512], start=(k == 0), stop=(k == KC - 1)))
                    tk.op("pe", fns, reads=["attnT", "o_wo"], writes=["pp%d" % b])
                    ky = "o_ysb%d" % b3
                    tk.op("act", lambda: nc.scalar.activation(out=ysb[b3][:], in_=pp[b][:], func=AF.Copy),
                          reads=["pp%d" % b], writes=[ky])
                    tk.op("dve", lambda: nc.vector.tensor_tensor(out=scr[:], in0=ysb[b3][:], in1=ysb[b3][:], op=ALU.mult),
                          reads=[ky], writes=["o_scr"])
                    tk.op("dve", lambda: nc.vector.tensor_reduce(out=st[:, 3 * b3:3 * b3 + 1], in_=scr[:], axis=AX.X, op=ALU.add),
                          reads=["o_scr"], writes=["o_ss%d" % b3])
                    rstd_pool(st[:, 3 * b3:3 * b3 + 1], st[:, 3 * b3 + 1:3 * b3 + 2], st[:, 3 * b3 + 2:3 * b3 + 3], 1, 1.0 / D,
                              "o_ss%d" % b3, "o_r%d" % b3)

                def o_back(tt):
                    b3 = tt % 3
                    ky = "o_ysb%d" % b3
                    tk.op("dve", lambda: nc.vector.scalar_tensor_tensor(out=ysb[b3][:], in0=ysb[b3][:], scalar=st[:, 3 * b3 + 2:3 * b3 + 3],
                                                                       in1=gpost[:], op0=ALU.mult, op1=ALU.mult),
                          reads=[ky, "o_r%d" % b3, "o_gpost"], writes=[ky])
                    tk.dma("sp", x2_d[tt * 128:(tt + 1) * 128, :], ysb[b3][:], reads=[ky], writes=[("x2", tt)])

                for tt in range(NT + 1):
                    if tt < NT:
                        o_front(tt)
                    if tt >= 1:
                        o_back(tt - 1)

        if "ffn1" in phases:
            ffn_phase(0, x_d, "xin", x1_d, "x1", 0, 1)
        if "ffn1out" in phases:
            ffn_phase(0, x_d, "xin", out_d, "out", 0, 1)
        tk.barrier()
        with nc.sbuf_tensor("s_attnT", [128, 8, T], BF16) as attnT_:
            mid["attnT"] = attnT_
            with (nc.sbuf_tensor("s_qaT", [128, NT, 4, 128], BF16) as qaT_,
                  nc.sbuf_tensor("s_qbT", [128, 4, T], BF16) as qbT_,
                  nc.sbuf_tensor("s_kaT", [128, 4, T], BF16) as kaT_,
                  nc.sbuf_tensor("s_va", [128, 64, 192], BF16) as va_):
                mid["qaT"], mid["qbT"] = qaT_, qbT_
                mid["kaT"], mid["va"] = kaT_, va_
                if "qkv" in phases:
                    qkv_phase()
                tk.barrier()
                if "attn" in phases:
                    attn_phase()
                tk.barrier()
            mid["qaT"] = mid["qbT"] = mid["kaT"] = mid["va"] = None
            with (nc.sbuf_tensor("f1_wg", [128, KC, FF], BF16) as wg2_,
                  nc.sbuf_tensor("f1_wu", [128, KC, FF], BF16) as wu2_,
                  nc.sbuf_tensor("f1_wd", [128, FC, D], BF16) as wd2_):
                if "outproj" in phases:
                    outproj_phase((lambda: ffn_weights(1, wg2_, wu2_, wd2_)) if "ffn2" in phases else None)
                elif "ffn2" in phases:
                    ffn_weights(1, wg2_, wu2_, wd2_)
                tk.barrier()
                if "ffn2" in phases:
                    ffn_phase(1, x2_d, "x2", out_d, "out", 4, 5, pre=(wg2_, wu2_, wd2_, attnT_[:].bitcast(F32)))
                tk.barrier()

        for tt in range(NT):
            tk._wait("sp", tk.last_w.get(("out", tt)))
    return nc


_NC_CACHE = {}


def _rope_tables(j):
    pos = np.arange(j * T, (j + 1) * T, dtype=np.int64)
    row = (pos // 64).astype(np.float64)
    col = (pos % 64).astype(np.float64)
    fa = 10000.0 ** (-np.arange(0, 32, 2, dtype=np.float64) / 32.0)
    fb = 10000.0 ** (-np.arange(0, 64, 2, dtype=np.float64) / 64.0)
    ang_r = row[:, None] * fa[None, :]
    ang_c = col[:, None] * fa[None, :]
    ang_b = pos.astype(np.float64)[:, None] * fb[None, :]
    ca = np.concatenate([np.cos(ang_r), np.cos(ang_c)], axis=1)
    sa = np.concatenate([np.sin(ang_r), np.sin(ang_c)], axis=1)
    return np.concatenate([ca, sa, np.cos(ang_b), np.sin(ang_b)], axis=1).astype(np.float32)


def _mask_b():
    kk = np.arange(128)[:, None, None]
    m = np.arange(NMASK)[None, :, None]
    qq = np.arange(512)[None, None, :]
    d = 128 * m - 256 + kk - qq
    c = (np.abs(d) <= 64).astype(np.float32)
    c += ((d % 4 == 0) & (np.abs(d) <= 256)).astype(np.float32)
    return c.astype(ml_dtypes.bfloat16)


def _mask_16():
    k = np.arange(128)[:, None]
    i = np.arange(128)[None, :]
    return np.stack([(k >= i), (k <= i)], axis=1).astype(np.float32).astype(ml_dtypes.bfloat16)


def kernel(x, ffn1_pre_g, ffn1_post_g, ffn1_w_gate, ffn1_w_up, ffn1_w_down,
           mix_pre_g, mix_post_g, w_qkv, q_norm_g, k_norm_g, w_out,
           ffn2_pre_g, ffn2_post_g, ffn2_w_gate, ffn2_w_up, ffn2_w_down):
    f = lambda a: np.ascontiguousarray(np.asarray(a, dtype=np.float32))
    x = f(x)
    hq = [0, 4, 1, 5, 2, 6, 3, 7]
    qa_cols = np.concatenate([np.arange(h * 64, (h + 1) * 64) for h in hq])
    cols = np.concatenate([qa_cols, np.arange(512, 2304)])
    wqkv = f(np.asarray(w_qkv)[0][:, cols])
    rows = np.concatenate([qa_cols, np.arange(512, 1024)])
    wout = f(np.asarray(w_out)[0][rows, :])
    gains = f(np.stack([np.asarray(g)[0] for g in (ffn1_pre_g, ffn1_post_g, mix_pre_g, mix_post_g, ffn2_pre_g, ffn2_post_g)]))
    gqk = f(np.stack([np.asarray(q_norm_g)[0], np.asarray(k_norm_g)[0]]))
    shared = {
        "wg1": f(np.asarray(ffn1_w_gate)[0]), "wu1": f(np.asarray(ffn1_w_up)[0]), "wd1": f(np.asarray(ffn1_w_down)[0]),
        "wg2": f(np.asarray(ffn2_w_gate)[0]), "wu2": f(np.asarray(ffn2_w_up)[0]), "wd2": f(np.asarray(ffn2_w_down)[0]),
        "wqkv": wqkv, "wout": wout, "gains": gains, "gqk": gqk,
        "maskb": _mask_b(), "mk2": _mask_16(), "ident": np.eye(128, dtype=np.float32).astype(ml_dtypes.bfloat16),
    }
    ropes = [_rope_tables(j) for j in range(4)]
    in_maps = []
    for c in range(NCORES):
        b, j = c // 4, c % 4
        m = dict(shared)
        m["x"] = np.ascontiguousarray(x[b, j * T:(j + 1) * T, :])
        m["rope"] = ropes[j]
        in_maps.append(m)
    if "nc" not in _NC_CACHE:
        _NC_CACHE["nc"] = build_nc()
    res = run_bass_kernel_spmd(_NC_CACHE["nc"], in_maps, core_ids=list(range(NCORES)))
    out = np.empty((2, SEQ, D), dtype=np.float32)
    for c in range(NCORES):
        b, j = c // 4, c % 4
        out[b, j * T:(j + 1) * T, :] = np.asarray(res.results[c]["out"], dtype=np.float32)
    return out
```

```python
import contextlib
import numpy as np
import ml_dtypes
import concourse.bass as bass
import concourse.mybir as mybir
from concourse.bass_utils import run_bass_kernel_spmd

F32 = mybir.dt.float32
BF16 = mybir.dt.bfloat16
AF = mybir.ActivationFunctionType
ALU = mybir.AluOpType
AX = mybir.AxisListType

NCORES = 8
T = 2048
NT = 16
D = 1024
KC = 8
FF = 2816
FC = 22
SEQ = 8192
EPS = 1e-6
SCALE = 0.125
NMASK = 8


class Trk:
    def __init__(self, nc, esems, dsems):
        self.nc = nc
        self.eng = {"pe": nc.tensor, "act": nc.scalar, "dve": nc.vector,
                    "pool": nc.gpsimd, "sp": nc.sync}
        self.sem = esems
        self.cnt = {k: 0 for k in esems}
        self.waited = {e: {} for e in self.eng}
        self.last_w = {}
        self.readers = {}
        self.dsems = dsems
        self.dcnt = [0] * len(dsems)
        self.dpool = {"sp": (0, 20), "act": (20, 28), "pool": (28, len(dsems))}
        self.drr = {"sp": 0, "act": 0, "pool": 0}

    def _wait(self, e, tok):
        if tok is None:
            return
        sem, val, src = tok
        if src == e and e == "pe":
            return
        k = id(sem)
        if self.waited[e].get(k, 0) >= val:
            return
        self.eng[e].wait_ge(sem, val)
        self.waited[e][k] = val

    def _deps(self, e, reads, writes):
        for b in reads:
            self._wait(e, self.last_w.get(b))
        for b in writes:
            self._wait(e, self.last_w.get(b))
            for t in self.readers.get(b, {}).values():
                self._wait(e, t)

    def _commit(self, tok, reads, writes):
        k = id(tok[0])
        for b in reads:
            self.readers.setdefault(b, {})[k] = tok
        for b in writes:
            self.last_w[b] = tok
            self.readers[b] = {}

    def op(self, e, fns, reads=(), writes=()):
        self._deps(e, reads, writes)
        if not isinstance(fns, (list, tuple)):
            fns = [fns]
        ins = None
        for f in fns:
            ins = f()
        self.cnt[e] += 1
        ins.then_inc(self.sem[e], 1)
        tok = (self.sem[e], self.cnt[e], e)
        self._commit(tok, reads, writes)
        return tok

    def dma(self, q, out, in_, reads=(), writes=()):
        self._deps(q, reads, writes)
        lo, hi = self.dpool[q]
        i = lo + self.drr[q]
        self.drr[q] = (self.drr[q] + 1) % (hi - lo)
        sem = self.dsems[i]
        if self.dcnt[i] > 0:
            self._wait(q, (sem, self.dcnt[i], "dma"))
        self.dcnt[i] += 16
        self.eng[q].dma_start(out=out, in_=in_).then_inc(sem, 16)
        tok = (sem, self.dcnt[i], "dma")
        self._commit(tok, reads, writes)
        return tok

    def barrier(self):
        for e in self.eng:
            for k, sem in self.sem.items():
                if self.cnt[k] > 0:
                    self._wait(e, (sem, self.cnt[k], "bar"))
            for i, sem in enumerate(self.dsems):
                if self.dcnt[i] > 0:
                    self._wait(e, (sem, self.dcnt[i], "dma"))

    def cc(self, sem, ins, outs, groups, reads=(), writes=()):
        e = "pool"
        self._deps(e, reads, writes)
        self.nc.gpsimd.collective_compute(
            "AllGather", ALU.bypass, replica_groups=groups, ins=ins, outs=outs
        ).then_inc(sem)
        tok = (sem, 1, "cc")
        self._commit(tok, reads, writes)
        return tok


def build_nc(debug=False, phases=("ffn1", "qkv", "attn", "outproj", "ffn2")):
    nc = bass.Bass("TRN2", target_bir_lowering=False)

    def din(name, shape, dt=F32):
        return nc.dram_tensor(name, list(shape), dt, kind="ExternalInput").ap()

    x_d = din("x", [T, D])
    wg_d = [din("wg1", [D, FF]), din("wg2", [D, FF])]
    wu_d = [din("wu1", [D, FF]), din("wu2", [D, FF])]
    wd_d = [din("wd1", [FF, D]), din("wd2", [FF, D])]
    wqkv_d = din("wqkv", [D, 2304])
    wout_d = din("wout", [D, D])
    gains_d = din("gains", [6, D])
    gqk_d = din("gqk", [2, 64])
    rope_d = din("rope", [T, 128])
    mask_d = din("maskb", [128, NMASK, 512], BF16)
    mk2_d = din("mk2", [128, 2, 128], BF16)
    ident_d = din("ident", [128, 128], BF16)
    out_d = nc.dram_tensor("out", [T, D], F32, kind="ExternalOutput").ap()

    x1_d = nc.dram_tensor("x1_park", [T, D], F32).ap()
    x2_d = nc.dram_tensor("x2_park", [T, D], F32).ap()
    xin_ka = nc.dram_tensor("xin_ka", [128, T], BF16)
    xin_va = nc.dram_tensor("xin_va", [T, 192], BF16)
    xin_kb = [nc.dram_tensor("xin_kb%d" % h, [256, T], BF16) for h in range(2)]
    xin_vb = [nc.dram_tensor("xin_vb%d" % p, [T, 192], BF16) for p in range(4)]
    g_ka = nc.dram_tensor("g_ka", [4 * 128, T], BF16)
    g_va = nc.dram_tensor("g_va", [4 * T, 192], BF16)
    big_kb = [nc.dram_tensor("big_kb%d" % h, [6 * 256, T], BF16) for h in range(2)]
    big_vb = [nc.dram_tensor("big_vb%d" % p, [6 * T, 192], BF16) for p in range(4)]
    win_kb = nc.dram_tensor("win_kb", [512, 4096], BF16)
    win_vb = nc.dram_tensor("win_vb", [4096, 768], BF16)

    groups = [[0, 1, 2, 3], [4, 5, 6, 7]]

    with contextlib.ExitStack() as es:
        def sb(name, shape, dt):
            return es.enter_context(nc.sbuf_tensor("s_" + name, list(shape), dt))

        esems = {k: es.enter_context(nc.semaphore("e_" + k)) for k in ("pe", "act", "dve", "pool")}
        dsems = [es.enter_context(nc.semaphore("d%d" % i)) for i in range(40)]
        ccs = [es.enter_context(nc.semaphore("cc%d" % i)) for i in range(8)]
        tk = Trk(nc, esems, dsems)
        pp = [es.enter_context(nc.psum_tensor("pp%d" % i, [128, 1024], F32)) for i in range(4)]

        ident = sb("ident", [128, 128], BF16)
        negh = sb("negh", [128, 8], F32)
        ones_bf = sb("ones_bf", [128, 64], BF16)
        zeros = sb("zeros", [128, 1024], BF16)
        tk.dma("sp", ident[:], ident_d, writes=["ident"])
        tk.op("pool", lambda: nc.gpsimd.memset(negh[:], -0.5), writes=["negh"])
        tk.op("pool", lambda: nc.gpsimd.memset(ones_bf[:], 1.0), writes=["ones"])
        tk.op("pool", lambda: nc.gpsimd.memset(zeros[:], 0.0), writes=["zeros"])
        def emit_pads():
            for slot, c0, r0 in ((0, 1024, 1024), (5, 0, 0)):
                for h in range(2):
                    tk.dma("act", big_kb[h][slot * 256:(slot + 1) * 256, c0:c0 + 1024].rearrange("(n p) t -> p n t", p=128),
                           zeros[:, 0:1024].unsqueeze(1).broadcast_to([128, 2, 1024]), reads=["zeros"], writes=[("big_kb_pad", h, slot)])
                for p in range(4):
                    tk.dma("act", big_vb[p][slot * T + r0:slot * T + r0 + 1024, :].rearrange("(n p) c -> p n c", p=128),
                           zeros[:, 0:192].unsqueeze(1).broadcast_to([128, 8, 192]), reads=["zeros"], writes=[("big_vb_pad", p, slot)])


        def rstd_pool(ss_ap, v_ap, r_ap, n, inv_n, key_ss, key_r, extra_scale=None):
            tk.op("pool", lambda: nc.gpsimd.tensor_scalar(out=v_ap, in0=ss_ap, scalar1=inv_n, scalar2=EPS,
                                                          op0=ALU.mult, op1=ALU.add),
                  reads=[key_ss], writes=[key_r + "_v"])
            tk.op("pool", lambda: nc.gpsimd.tensor_tensor(out=r_ap, in0=v_ap, in1=negh[:, 0:n], op=ALU.pow),
                  reads=[key_r + "_v", "negh"], writes=[key_r])

        def load_gain(dst, idx, key, half=False):
            tk.dma("sp", dst[:], gains_d[idx:idx + 1, :].broadcast_to([128, D]), writes=[key])
            if half:
                tk.op("dve", lambda: nc.vector.tensor_scalar(out=dst[:], in0=dst[:], scalar1=0.5, scalar2=None,
                                                             op0=ALU.mult), reads=[key], writes=[key])

        cgrp = [(0, 2), (2, 6), (6, 14), (14, 22)]

        def ffn_weights(pi, wg, wu, wd):
            P = "f%d_" % pi
            wgv = wg_d[pi].rearrange("(k p) f -> p k f", p=128)
            wuv = wu_d[pi].rearrange("(k p) f -> p k f", p=128)
            wdv = wd_d[pi].rearrange("(c p) n -> p c n", p=128)
            for g, (lo, hi) in enumerate(cgrp):
                fs_ = slice(lo * 128, hi * 128)
                tk.dma("pool", wg[:, :, fs_], wgv[:, :, fs_], writes=[P + "wg%d" % g])
                tk.dma("pool", wu[:, :, fs_], wuv[:, :, fs_], writes=[P + "wu%d" % g])
                tk.dma("pool", wd[:, lo:hi, :], wdv[:, lo:hi, :], writes=[P + "wd%d" % g])

        def ffn_phase(pi, src_d, src_key, dst_d, dst_key, gpre_i, gpost_i, pre=None):
            with contextlib.ExitStack() as fs:
                def fsb(name, shape, dt):
                    return fs.enter_context(nc.sbuf_tensor("f%d_%s" % (pi, name), list(shape), dt))
                if pre is None:
                    wg = fsb("wg", [128, KC, FF], BF16)
                    wu = fsb("wu", [128, KC, FF], BF16)
                    wd = fsb("wd", [128, FC, D], BF16)
                else:
                    wg, wu, wd, af = pre
                gpre = fsb("gpre", [128, D], F32)
                gpost = fsb("gpost", [128, D], F32)
                if pre is None:
                    xt = [fsb("xt%d" % i, [128, D], F32) for i in range(4)]
                    scr = fsb("scr", [128, D], F32)
                else:
                    xt = [af[:, i, :] for i in range(4)]
                    scr = af[:, 6, :]
                    dl = [af[:, 7, :], fsb("dl1", [128, D], F32)]
                xn = [fsb("xn%d" % i, [128, D], BF16) for i in range(2)]
                hT = [fsb("hT%d" % i, [128, KC, 256], BF16) for i in range(2)]
                sg = [fsb("sg%d" % i, [128, 256], F32) for i in range(2)]
                aT = [fsb("aT%d" % i, [128, 256], BF16) for i in range(4)]
                if pre is None:
                    ysb = [fsb("ysb%d" % i, [128, D], F32) for i in range(2)]
                else:
                    ysb = [af[:, 4 + i, :] for i in range(2)]
                st = fsb("st", [128, 16], F32)

                P = "f%d_" % pi
                load_gain(gpre, gpre_i, P + "gpre")
                load_gain(gpost, gpost_i, P + "gpost", half=True)
                if pre is None:
                    ffn_weights(pi, wg, wu, wd)
                grp_of = {}
                for g, (lo, hi) in enumerate(cgrp):
                    for c in range(lo, hi):
                        grp_of[c] = g

                pst = pp[3][:, 0:512].bitcast(BF16)

                def prepA0(blk, t2):
                    tt = blk * 2 + t2
                    xb = tt % 4
                    kx = P + "xt%d" % xb
                    tk.dma("sp", xt[xb][:], src_d[tt * 128:(tt + 1) * 128, :],
                           reads=[(src_key, tt)], writes=[kx])
                    if pre is not None:
                        tk.dma("sp", dl[t2][:], x1_d[tt * 128:(tt + 1) * 128, :], reads=[("x1", tt)], writes=[P + "dl%d" % t2])

                def prepA1(blk, t2):
                    tt = blk * 2 + t2
                    xb = tt % 4
                    kx = P + "xt%d" % xb
                    c0 = 3 * t2
                    if pre is not None:
                        tk.op("dve", lambda: nc.vector.tensor_tensor(out=xt[xb][:], in0=xt[xb][:], in1=dl[t2][:], op=ALU.add),
                              reads=[kx, P + "dl%d" % t2], writes=[kx])
                    tk.op("dve", lambda: nc.vector.tensor_tensor(out=scr[:], in0=xt[xb][:], in1=xt[xb][:], op=ALU.mult),
                          reads=[kx], writes=[P + "scr"])
                    tk.op("dve", lambda: nc.vector.tensor_reduce(out=st[:, c0:c0 + 1], in_=scr[:], axis=AX.X, op=ALU.add),
                          reads=[P + "scr"], writes=[P + "ss%d" % t2])
                    rstd_pool(st[:, c0:c0 + 1], st[:, c0 + 1:c0 + 2], st[:, c0 + 2:c0 + 3], 1, 1.0 / D, P + "ss%d" % t2, P + "r%d" % t2)

                def prepA2(blk, t2):
                    tt = blk * 2 + t2
                    xb = tt % 4
                    kx = P + "xt%d" % xb
                    c0 = 3 * t2
                    xnb = tt % 2
                    tk.op("dve", lambda: nc.vector.scalar_tensor_tensor(out=xn[xnb][:], in0=xt[xb][:], scalar=st[:, c0 + 2:c0 + 3],
                                                                       in1=gpre[:], op0=ALU.mult, op1=ALU.mult),
                          reads=[kx, P + "r%d" % t2, P + "gpre"], writes=[P + "xn%d" % xnb])

                def prepB(blk):
                    hb = blk % 2
                    for t2 in range(2):
                        tt = blk * 2 + t2
                        xnb = tt % 2
                        tk.op("pe", [(lambda k=k: nc.tensor.transpose(pst[:, k * 128:(k + 1) * 128],
                                                                     xn[xnb][:, k * 128:(k + 1) * 128], ident[:]))
                                     for k in range(KC)],
                              reads=[P + "xn%d" % xnb, "ident"], writes=["pp3a"])
                        tk.op("act", lambda: nc.scalar.activation(out=hT[hb][:, :, t2 * 128:(t2 + 1) * 128],
                                                                   in_=pst.rearrange("p (k t) -> p k t", k=KC),
                                                                   func=AF.Copy),
                              reads=["pp3a"], writes=[P + "hT%d_%d" % (hb, t2)])

                def down(blk, fc):
                    ab = fc % 4
                    fns = []
                    for t2 in range(2):
                        for hf in range(2):
                            fns.append(lambda t2=t2, hf=hf: nc.tensor.matmul(
                                pp[t2][:, hf * 512:(hf + 1) * 512],
                                lhsT=aT[ab][:, t2 * 128:(t2 + 1) * 128],
                                rhs=wd[:, fc, hf * 512:(hf + 1) * 512],
                                start=(fc == 0), stop=(fc == FC - 1)))
                    tk.op("pe", fns, reads=[P + "aT%d" % ab, P + "wd%d" % grp_of[fc]], writes=["pp0", "pp1"])

                def post0(blk):
                    for t2 in range(2):
                        tt = blk * 2 + t2
                        yb = tt % 2
                        tk.op("act", lambda: nc.scalar.activation(out=ysb[yb][:], in_=pp[t2][:], func=AF.Copy),
                              reads=["pp%d" % t2], writes=[P + "ysb%d" % yb])

                def postA(blk, t2):
                    tt = blk * 2 + t2
                    yb = tt % 2
                    ky = P + "ysb%d" % yb
                    c0 = 6 + 3 * t2
                    tk.op("dve", lambda: nc.vector.tensor_tensor(out=scr[:], in0=ysb[yb][:], in1=ysb[yb][:], op=ALU.mult),
                          reads=[ky], writes=[P + "scr"])
                    tk.op("dve", lambda: nc.vector.tensor_reduce(out=st[:, c0:c0 + 1], in_=scr[:], axis=AX.X, op=ALU.add),
                          reads=[P + "scr"], writes=[P + "pss%d" % t2])
                    rstd_pool(st[:, c0:c0 + 1], st[:, c0 + 1:c0 + 2], st[:, c0 + 2:c0 + 3], 1, 1.0 / D, P + "pss%d" % t2, P + "pr%d" % t2)

                def postB(blk, t2):
                    tt = blk * 2 + t2
                    xb = tt % 4
                    yb = tt % 2
                    ky = P + "ysb%d" % yb
                    c0 = 6 + 3 * t2
                    tk.op("dve", lambda: nc.vector.scalar_tensor_tensor(out=ysb[yb][:], in0=ysb[yb][:], scalar=st[:, c0 + 2:c0 + 3],
                                                                       in1=gpost[:], op0=ALU.mult, op1=ALU.mult),
                          reads=[ky, P + "pr%d" % t2, P + "gpost"], writes=[ky])
                    tk.op("dve", lambda: nc.vector.tensor_tensor(out=ysb[yb][:], in0=ysb[yb][:], in1=xt[xb][:], op=ALU.add),
                          reads=[ky, P + "xt%d" % xb], writes=[ky])
                    tk.dma("sp", dst_d[tt * 128:(tt + 1) * 128, :], ysb[yb][:], reads=[ky], writes=[(dst_key, tt)])

                nblk = NT // 2
                for t2 in range(2):
                    prepA0(0, t2)
                for t2 in range(2):
                    prepA1(0, t2)
                for t2 in range(2):
                    prepA2(0, t2)
                prepB(0)
                for blk in range(nblk):
                    hb = blk % 2
                    for fc in range(FC):
                        s = fc % 2
                        g = grp_of[fc]
                        fsl = slice(fc * 128, (fc + 1) * 128)
                        gu = pp[2][:, s * 512:(s + 1) * 512]
                        fns = []
                        for k in range(KC):
                            fns.append(lambda k=k: nc.tensor.matmul(gu[:, 0:256], lhsT=wg[:, k, fsl], rhs=hT[hb][:, k, :],
                                                                    start=(k == 0), stop=(k == KC - 1)))
                        for k in range(KC):
                            fns.append(lambda k=k: nc.tensor.matmul(gu[:, 256:512], lhsT=wu[:, k, fsl], rhs=hT[hb][:, k, :],
                                                                    start=(k == 0), stop=(k == KC - 1)))
                        tk.op("pe", fns, reads=[P + "wg%d" % g, P + "wu%d" % g, P + "hT%d_0" % hb, P + "hT%d_1" % hb],
                              writes=["pp2_%d" % s])
                        tk.op("act", lambda: nc.scalar.activation(out=sg[s][:], in_=gu[:, 0:256], func=AF.Silu),
                              reads=["pp2_%d" % s], writes=[P + "sg%d" % s])
                        ab = fc % 4
                        tk.op("dve", lambda: nc.vector.tensor_tensor(out=aT[ab][:], in0=sg[s][:], in1=gu[:, 256:512], op=ALU.mult),
                              reads=[P + "sg%d" % s, "pp2_%d" % s], writes=[P + "aT%d" % ab])
                        if fc >= 1:
                            down(blk, fc - 1)
                        if blk >= 1:
                            if fc in (1, 2):
                                postA(blk - 1, fc - 1)
                            if fc in (3, 4):
                                postB(blk - 1, fc - 3)
                        if blk + 1 < nblk:
                            if fc == 5:
                                prepA0(blk + 1, 0)
                                prepA0(blk + 1, 1)
                            if fc in (7, 8):
                                prepA1(blk + 1, fc - 7)
                            if fc in (9, 10):
                                prepA2(blk + 1, fc - 9)
                            if fc == 13:
                                prepB(blk + 1)
                        if pi == 0 and blk == 4 and fc == 16:
                            emit_pads()
                    down(blk, FC - 1)
                    post0(blk)
                for t2 in range(2):
                    postA(nblk - 1, t2)
                for t2 in range(2):
                    postB(nblk - 1, t2)

        mid = {}

        def qkv_phase():
            qaT, qbT, attnT = mid["qaT"], mid["qbT"], mid["attnT"]
            kaT_all, va_all = mid["kaT"], mid["va"]
            with contextlib.ExitStack() as fs:
                def fsb(name, shape, dt):
                    return fs.enter_context(nc.sbuf_tensor("q_" + name, list(shape), dt))
                wq = fsb("wq", [128, KC, 2304], BF16)
                gpre = fsb("gpre", [128, D], F32)
                gq = fsb("gq", [128, 2, 64], F32)
                rope = fsb("rope", [128, NT, 128], F32)
                xt = [fsb("xt%d" % i, [128, D], F32) for i in range(2)]
                xn = [fsb("xn%d" % i, [128, D], BF16) for i in range(2)]
                scr = fsb("scr", [128, D], F32)
                hT = mid["attnT"]
                st = fsb("st", [128, 64], F32)
                stg = [fsb("stg%d" % i, [128, 1536], F32) for i in range(2)]
                kvs = [fsb("kvs%d" % i, [128, 128], F32) for i in range(3)]
                t1 = fsb("t1", [128, 512], F32)
                t2b = fsb("t2", [128, 512], F32)
                t3 = fsb("t3", [128, 512], F32)
                t4 = fsb("t4", [128, 512], F32)
                t1p = t2p = t3p = t4p = None
                qr = [fsb("qr%d" % i, [128, 1536], BF16) for i in range(2)]
                kr = [fsb("kr%d" % i, [128, 128], BF16) for i in range(2)]
                kas = [fsb("kas%d" % i, [128, 128], BF16) for i in range(2)]
                kbs = [fsb("kbs%d" % i, [128, 4, 128], BF16) for i in range(2)]
                vax = [fsb("vax%d" % i, [128, 192], BF16) for i in range(2)]
                vbx = [fsb("vbx%d" % i, [128, 768], BF16) for i in range(2)]

                load_gain(gpre, 2, "q_gpre")
                tk.dma("sp", gq[:].rearrange("p a d -> p (a d)"),
                       gqk_d.rearrange("a d -> (a d)").unsqueeze(0).broadcast_to([128, 128]), writes=["q_gq"])
                tk.dma("sp", rope[:], rope_d.rearrange("(n p) c -> p n c", p=128), writes=["q_rope"])
                wqv = wqkv_d.rearrange("(k p) f -> p k f", p=128)
                tk.dma("pool", wq[:, :, 512:768], wqv[:, :, 512:768], writes=["q_wq_kv"])
                tk.dma("pool", wq[:, :, 0:512], wqv[:, :, 0:512], writes=["q_wq_qa"])
                for g in range(3):
                    tk.dma("pool", wq[:, :, 768 + g * 512:768 + (g + 1) * 512], wqv[:, :, 768 + g * 512:768 + (g + 1) * 512],
                           writes=["q_wq_b%d" % g])
                for i in range(2):
                    tk.op("pool", lambda i=i: nc.gpsimd.memset(vax[i][:], 1.0), writes=["q_vax%d" % i])
                    tk.op("pool", lambda i=i: nc.gpsimd.memset(vbx[i][:], 1.0), writes=["q_vbx%d" % i])

                def rope_ops(e, src, src_keys, nh, cos, sin, axial, dst, dst_key):
                    E = nc.vector if e == "dve" else nc.gpsimd
                    scrs = (t1, t2b, t3, t4) if e == "dve" else (t1p, t2p, t3p, t4p)
                    tkey = "q_t" if e == "dve" else "q_tp"
                    if axial:
                        sv = src.rearrange("p (h a b d) -> p h a b d", h=nh, a=2, b=2, d=16)
                        dv = dst.rearrange("p (h a b d) -> p h a b d", h=nh, a=2, b=2, d=16)
                        x1, x2 = sv[:, :, :, 0, :], sv[:, :, :, 1, :]
                        d1, d2 = dv[:, :, :, 0, :], dv[:, :, :, 1, :]
                        cb = cos.rearrange("p (a d) -> p a d", a=2).unsqueeze(1).broadcast_to([128, nh, 2, 16])
                        sn = sin.rearrange("p (a d) -> p a d", a=2).unsqueeze(1).broadcast_to([128, nh, 2, 16])
                        ta, tb, tc, td = [t_[:, 0:nh * 32].rearrange("p (h a d) -> p h a d", h=nh, a=2) for t_ in scrs]
                    else:
                        sv = src.rearrange("p (h b d) -> p h b d", h=nh, b=2, d=32)
                        dv = dst.rearrange("p (h b d) -> p h b d", h=nh, b=2, d=32)
                        x1, x2 = sv[:, :, 0, :], sv[:, :, 1, :]
                        d1, d2 = dv[:, :, 0, :], dv[:, :, 1, :]
                        cb = cos.unsqueeze(1).broadcast_to([128, nh, 32])
                        sn = sin.unsqueeze(1).broadcast_to([128, nh, 32])
                        ta, tb, tc, td = [t_[:, 0:nh * 32].rearrange("p (h d) -> p h d", h=nh) for t_ in scrs]
                    rk = list(src_keys) + ["q_rope"]
                    k1, k2, k3, k4 = [tkey + str(i_) for i_ in range(4)]
                    tk.op(e, lambda: E.tensor_tensor(out=ta, in0=x1, in1=cb, op=ALU.mult), reads=rk, writes=[k1])
                    tk.op(e, lambda: E.tensor_tensor(out=tb, in0=x2, in1=sn, op=ALU.mult), reads=rk, writes=[k2])
                    tk.op(e, lambda: E.tensor_tensor(out=tc, in0=x2, in1=cb, op=ALU.mult), reads=rk, writes=[k3])
                    tk.op(e, lambda: E.tensor_tensor(out=td, in0=x1, in1=sn, op=ALU.mult), reads=rk, writes=[k4])
                    tk.op(e, lambda: E.tensor_tensor(out=d1, in0=ta, in1=tb, op=ALU.subtract), reads=[k1, k2], writes=[dst_key])
                    tk.op(e, lambda: E.tensor_tensor(out=d2, in0=tc, in1=td, op=ALU.add), reads=[k3, k4], writes=[dst_key])

                def headnorm_a(dstv, key, nh, gi, stc):
                    n = nh * 64
                    sc_ = scr
                    sk_ = "q_scr"
                    tk.op("dve", lambda: nc.vector.tensor_tensor(out=sc_[:, 0:n], in0=dstv, in1=dstv, op=ALU.mult),
                          reads=[key], writes=[sk_])
                    tk.op("dve", lambda: nc.vector.tensor_reduce(out=st[:, stc:stc + nh],
                                                                 in_=sc_[:, 0:n].rearrange("p (h d) -> p h d", h=nh),
                                                                 axis=AX.X, op=ALU.add),
                          reads=[sk_], writes=["q_ssq%d" % stc])
                    rstd_pool(st[:, stc:stc + nh], st[:, stc + 8:stc + 8 + nh], st[:, stc + 16:stc + 16 + nh], nh, 1.0 / 64,
                              "q_ssq%d" % stc, "q_rq%d" % stc)

                def headnorm_b(dstv, key, nh, gi, stc):
                    d3 = dstv.rearrange("p (h d) -> p h d", h=nh)
                    tk.op("dve", lambda: nc.vector.tensor_tensor(
                        out=d3, in0=d3, in1=st[:, stc + 16:stc + 16 + nh].unsqueeze(2).broadcast_to([128, nh, 64]), op=ALU.mult),
                        reads=[key, "q_rq%d" % stc], writes=[key])
                    tk.op("dve", lambda: nc.vector.tensor_tensor(
                        out=d3, in0=d3, in1=gq[:, gi, :].unsqueeze(1).broadcast_to([128, nh, 64]), op=ALU.mult),
                        reads=[key, "q_gq"], writes=[key])

                pst = [pp[3][:, 0:512].bitcast(BF16), pp[3][:, 512:1024].bitcast(BF16)]
                pkv = [pp[0][:, 0:256], pp[0][:, 512:768]]
                pko = [pp[1][:, 0:512].bitcast(BF16), pp[1][:, 512:1024].bitcast(BF16)]

                def p1_load(tt):
                    xb = tt % 2
                    tk.dma("sp", xt[xb][:], x1_d[tt * 128:(tt + 1) * 128, :], reads=[("x1", tt)], writes=["q_xt%d" % xb])

                def p1_B(tt):
                    xb = tt % 2
                    c0 = 3 * xb
                    tk.op("dve", lambda: nc.vector.tensor_tensor(out=scr[:], in0=xt[xb][:], in1=xt[xb][:], op=ALU.mult),
                          reads=["q_xt%d" % xb], writes=["q_scr"])
                    tk.op("dve", lambda: nc.vector.tensor_reduce(out=st[:, c0:c0 + 1], in_=scr[:], axis=AX.X, op=ALU.add),
                          reads=["q_scr"], writes=["q_ss%d" % xb])
                    rstd_pool(st[:, c0:c0 + 1], st[:, c0 + 1:c0 + 2], st[:, c0 + 2:c0 + 3], 1, 1.0 / D, "q_ss%d" % xb, "q_r%d" % xb)

                def p1_E(tt):
                    xb = tt % 2
                    c0 = 3 * xb
                    tk.op("dve", lambda: nc.vector.scalar_tensor_tensor(out=xn[xb][:], in0=xt[xb][:], scalar=st[:, c0 + 2:c0 + 3],
                                                                       in1=gpre[:], op0=ALU.mult, op1=ALU.mult),
                          reads=["q_xt%d" % xb, "q_r%d" % xb, "q_gpre"], writes=["q_xn%d" % xb])
                    tk.op("pe", [(lambda k=k: nc.tensor.transpose(pst[xb][:, k * 128:(k + 1) * 128],
                                                                 xn[xb][:, k * 128:(k + 1) * 128], ident[:]))
                                 for k in range(KC)], reads=["q_xn%d" % xb, "ident"], writes=["q_pst%d" % xb])
                    tk.op("act", lambda: nc.scalar.activation(out=hT[:, :, tt * 128:(tt + 1) * 128],
                                                               in_=pst[xb].rearrange("p (k t) -> p k t", k=KC), func=AF.Copy),
                          reads=["q_pst%d" % xb], writes=[("q_hT", tt)])

                def p1_C(tt):
                    b = tt % 2
                    b3 = tt % 3
                    tk.op("pe", [(lambda k=k: nc.tensor.matmul(pkv[b], lhsT=hT[:, k, tt * 128:(tt + 1) * 128], rhs=wq[:, k, 512:768],
                                                               start=(k == 0), stop=(k == KC - 1))) for k in range(KC)],
                          reads=[("q_hT", tt), "q_wq_kv"], writes=["q_pkv%d" % b])
                    tk.op("act", lambda: nc.scalar.activation(out=kvs[b3][:], in_=pkv[b][:, 0:128], func=AF.Copy),
                          reads=["q_pkv%d" % b], writes=["q_kvs%d" % b3])
                    tk.op("act", lambda: nc.scalar.activation(
                        out=vax[b][:].rearrange("p (s d) -> p s d", s=3)[:, 0:3:2, :],
                        in_=pkv[b][:, 128:256].rearrange("p (s d) -> p s d", s=2), func=AF.Copy),
                        reads=["q_pkv%d" % b], writes=["q_vax%d" % b])
                    tk.dma("sp", xin_va[tt * 128:(tt + 1) * 128, :], vax[b][:], reads=["q_vax%d" % b], writes=[("xin_va", tt)])

                def p1_D(tt):
                    b3 = tt % 3
                    headnorm_a(kvs[b3][:], "q_kvs%d" % b3, 2, 1, 32 + 2 * b3)

                def p1_F(tt):
                    b = tt % 2
                    b3 = tt % 3
                    headnorm_b(kvs[b3][:], "q_kvs%d" % b3, 2, 1, 32 + 2 * b3)
                    rope_ops("dve", kvs[b3][:], ["q_kvs%d" % b3], 2, rope[:, tt, 0:32], rope[:, tt, 32:64], True,
                             kr[b][:], "q_kr%d" % b)
                    tk.op("pe", lambda: nc.tensor.transpose(pko[b][:, 0:128], kr[b][:], ident[:]),
                          reads=["q_kr%d" % b, "ident"], writes=["q_pko%d" % b])
                    tk.op("act", lambda: nc.scalar.activation(out=kas[b][:], in_=pko[b][:, 0:128], func=AF.Copy),
                          reads=["q_pko%d" % b], writes=["q_kas%d" % b])
                    tk.dma("sp", xin_ka[:, tt * 128:(tt + 1) * 128], kas[b][:], reads=["q_kas%d" % b], writes=[("xin_ka", tt)])

                p1_load(0)
                p1_load(1)
                p1_B(0)
                p1_E(0)
                for tt in range(NT):
                    if tt + 1 < NT:
                        p1_B(tt + 1)
                    p1_C(tt)
                    p1_D(tt)
                    if tt >= 1:
                        p1_F(tt - 1)
                    if tt + 1 < NT:
                        p1_E(tt + 1)
                    if tt + 2 < NT:
                        p1_load(tt + 2)
                p1_F(NT - 1)
                tk.cc(ccs[0], [xin_ka.ap().opt()], [g_ka.ap().opt()], groups, reads=[("xin_ka", t_) for t_ in range(NT)], writes=["g_ka"])
                tk.cc(ccs[1], [xin_va.ap().opt()], [g_va.ap().opt()], groups, reads=[("xin_va", t_) for t_ in range(NT)], writes=["g_va"])
                tk.dma("sp", kaT_all[:], g_ka.ap().rearrange("(r p) t -> p r t", p=128), reads=["g_ka"], writes=["a_kaT"])
                for h in range(4):
                    for h2 in range(2):
                        r0 = h * T + h2 * 1024
                        tk.dma("sp", va_all[:, h * 16 + h2 * 8:h * 16 + h2 * 8 + 8, :],
                               g_va[r0:r0 + 1024, :].rearrange("(n p) c -> p n c", p=128), reads=["g_va"],
                               writes=[("a_va", h * 2 + h2)])

                tk._wait("pe", (tk.sem["act"], tk.cnt["act"], "bar"))
                grp = [(pp[0][:, 0:512], slice(0, 512), "q_wq_qa", "q_pg0"),
                       (pp[0][:, 512:1024], slice(768, 1280), "q_wq_b0", "q_pg1"),
                       (pp[1][:, 0:512], slice(1280, 1792), "q_wq_b1", "q_pg2"),
                       (pp[1][:, 512:1024], slice(1792, 2304), "q_wq_b2", "q_pg3")]
                pso = pp[2][:].bitcast(BF16)
                def mm2(tt):
                    b = tt % 2
                    for gi, (pap, csl, wkey, pkey) in enumerate(grp):
                        tk.op("pe", [(lambda k=k, pap=pap, csl=csl: nc.tensor.matmul(pap, lhsT=hT[:, k, tt * 128:(tt + 1) * 128],
                                                                                   rhs=wq[:, k, csl], start=(k == 0), stop=(k == KC - 1)))
                                     for k in range(KC)],
                              reads=[("q_hT", tt), wkey], writes=[pkey])
                        if gi < 3:
                            tk.op("act", lambda gi=gi, pap=pap: nc.scalar.activation(out=stg[b][:, gi * 512:(gi + 1) * 512], in_=pap,
                                                                                   func=AF.Copy),
                                  reads=[pkey], writes=[("q_stg", b, gi)])
                        else:
                            tk.op("act", lambda: nc.scalar.activation(
                                out=vbx[b][:].rearrange("p (r s d) -> p r s d", r=4, s=3)[:, :, 0:3:2, :],
                                in_=pap.rearrange("p (r s d) -> p r s d", r=4, s=2), func=AF.Copy),
                                reads=[pkey], writes=["q_vbx%d" % b])
                            for p in range(4):
                                tk.dma("act", xin_vb[p][tt * 128:(tt + 1) * 128, :], vbx[b][:, p * 192:(p + 1) * 192],
                                       reads=["q_vbx%d" % b], writes=[("xin_vb", p, tt)])

                def chain2(tt):
                    b = tt % 2
                    cA, sA = rope[:, tt, 0:32], rope[:, tt, 32:64]
                    cB, sB = rope[:, tt, 64:96], rope[:, tt, 96:128]
                    headnorm_a(stg[b][:, 0:512], ("q_stg", b, 0), 8, 0, 8)
                    rope_ops("dve", stg[b][:, 512:1536], [("q_stg", b, 1), ("q_stg", b, 2)], 16, cB, sB, False,
                             qr[b][:, 512:1536], ("q_qr", b, 1))
                    headnorm_b(stg[b][:, 0:512], ("q_stg", b, 0), 8, 0, 8)
                    rope_ops("dve", stg[b][:, 0:512], [("q_stg", b, 0)], 8, cA, sA, True, qr[b][:, 0:512], ("q_qr", b, 0))
                    tk.op("pe", [(lambda i=i: nc.tensor.transpose(pso[:, i * 128:(i + 1) * 128],
                                                                 qr[b][:, i * 128:(i + 1) * 128], ident[:]))
                                 for i in range(12)],
                          reads=[("q_qr", b, 0), ("q_qr", b, 1), "ident"], writes=["q_pso"])
                    tk.op("act", lambda: nc.scalar.activation(out=qaT[:, tt, :, :],
                                                               in_=pso[:, 0:512].rearrange("p (g q) -> p g q", g=4), func=AF.Copy),
                          reads=["q_pso"], writes=["qaT"])
                    tk.op("act", lambda: nc.scalar.activation(out=qbT[:, :, tt * 128:(tt + 1) * 128],
                                                               in_=pso[:, 512:1024].rearrange("p (g q) -> p g q", g=4), func=AF.Copy),
                          reads=["q_pso"], writes=["qbT"])
                    tk.op("act", lambda: nc.scalar.activation(out=kbs[b][:],
                                                               in_=pso[:, 1024:1536].rearrange("p (g q) -> p g q", g=4), func=AF.Copy),
                          reads=["q_pso"], writes=["q_kbs%d" % b])
                    for h in range(2):
                        tk.dma("act", xin_kb[h].ap().rearrange("(g p) t -> p g t", p=128)[:, :, tt * 128:(tt + 1) * 128],
                               kbs[b][:, 2 * h:2 * h + 2, :], reads=["q_kbs%d" % b], writes=[("xin_kb", h, tt)])

                mm2(0)
                for tt in range(NT):
                    if tt + 1 < NT:
                        mm2(tt + 1)
                    chain2(tt)
                if "nocc" in phases:
                    return
                for h in range(2):
                    tk.cc(ccs[2 + h], [xin_kb[h].ap().opt()], [big_kb[h][256:5 * 256, :]], groups,
                          reads=[("xin_kb", h, t_) for t_ in range(NT)], writes=["big_kb%d" % h])
                for p in range(4):
                    tk.cc(ccs[4 + p], [xin_vb[p].ap().opt()], [big_vb[p][T:5 * T, :]], groups,
                          reads=[("xin_vb", p, t_) for t_ in range(NT)], writes=["big_vb%d" % p])

        def normalize1(po, okey, Osb):
            tk.op("dve", lambda: nc.vector.tensor_copy(out=Osb[:], in_=po[:]), reads=[okey], writes=["Osb"])

        def normalize2(dst0, dst1, R, Osb, Rl, use_act):
            g3 = len(dst0.shape) == 3
            def v(ap):
                return ap.rearrange("p (g q) -> p g q", g=dst0.shape[1]) if g3 else ap
            if use_act:
                tk.op("act", lambda: nc.scalar.activation(out=Rl[64:128, :], in_=Osb[64:128, 0:512], func=AF.Ln),
                      reads=["Osb"], writes=["Rl0"])
                tk.op("act", lambda: nc.scalar.activation(out=Rl[64:128, :], in_=Rl[64:128, :], func=AF.Exp, scale=-1.0),
                      reads=["Rl0"], writes=["Rl0"])
                tk.op("act", lambda: nc.scalar.activation(out=Rl[0:64, :], in_=Osb[0:64, 512:1024], func=AF.Ln),
                      reads=["Osb"], writes=["Rl1"])
                tk.op("act", lambda: nc.scalar.activation(out=Rl[0:64, :], in_=Rl[0:64, :], func=AF.Exp, scale=-1.0),
                      reads=["Rl1"], writes=["Rl1"])
                tk.op("dve", lambda: nc.vector.tensor_copy(out=R[0:64, :], in_=Rl[64:128, :]), reads=["Rl0"], writes=["R0"])
                tk.op("dve", lambda: nc.vector.tensor_copy(out=R[64:128, :], in_=Rl[0:64, :]), reads=["Rl1"], writes=["R1"])
            else:
                tk.op("dve", lambda: nc.vector.reciprocal(out=R[0:64, :], in_=Osb[64:128, 0:512]), reads=["Osb"], writes=["R0"])
                tk.op("dve", lambda: nc.vector.reciprocal(out=R[64:128, :], in_=Osb[0:64, 512:1024]), reads=["Osb"], writes=["R1"])
            tk.op("dve", lambda: nc.vector.tensor_tensor(out=dst0, in0=v(Osb[0:64, 0:512]), in1=v(R[0:64, :]), op=ALU.mult),
                  reads=["Osb", "R0"], writes=["attnT"])
            tk.op("dve", lambda: nc.vector.tensor_tensor(out=dst1, in0=v(Osb[64:128, 512:1024]), in1=v(R[64:128, :]), op=ALU.mult),
                  reads=["Osb", "R1"], writes=["attnT"])

        def attn_phase():
            qaT, qbT, attnT = mid["qaT"], mid["qbT"], mid["attnT"]
            kaT, va = mid["kaT"], mid["va"]
            NP = 5
            with contextlib.ExitStack() as fs:
                def fsb(name, shape, dt):
                    return fs.enter_context(nc.sbuf_tensor("a_" + name, list(shape), dt))
                Pb = [fsb("P%d" % i, [128, 1024], BF16) for i in range(NP)]
                R = fsb("R", [128, 512], F32)
                Osb = fsb("Osb", [128, 1024], F32)
                Rl = fsb("Rl", [128, 512], F32)
                maskb = fsb("maskb", [128, NMASK, 512], BF16)
                kbw = [fsb("kbw%d" % i, [128, 4096], BF16) for i in range(2)]
                vbw = [fsb("vbw%d" % i, [128, 20, 192], BF16) for i in range(2)]
                vbs = [fsb("vbs%d" % i, [128, 16, 2, 192], BF16) for i in range(2)]
                mk2 = fsb("mk2", [128, 2, 128], BF16)
                O2sb = fsb("O2sb", [128, 4, 1024], F32)

                tk.dma("sp", maskb[:], mask_d, writes=["a_mask"])
                tk.dma("sp", mk2[:], mk2_d, writes=["a_mk2"])
                va_keys = [("a_va", h) for h in range(8)]
                j = nc.sync.partition_id() % 4
                po = pp[3]

                steps = [(qb, kt) for qb in range(NT) for kt in range(64)]
                n = len(steps)

                def qk_a(i):
                    qb, kt = steps[i]
                    s = i % 3
                    ps = pp[s]
                    kr_, kc = kt // 16, (kt % 16) * 128
                    tk.op("pe", [lambda: nc.tensor.matmul(ps[:, 0:512], lhsT=kaT[0:64, kr_, kc:kc + 128],
                                                          rhs=qaT[0:64, qb, :, :].rearrange("p g q -> p (g q)"),
                                                          start=True, stop=True),
                                 lambda: nc.tensor.matmul(ps[:, 512:1024], lhsT=kaT[64:128, kr_, kc:kc + 128],
                                                          rhs=qaT[64:128, qb, :, :].rearrange("p g q -> p (g q)"),
                                                          start=True, stop=True)],
                          reads=["a_kaT", "qaT"], writes=["ppS%d" % s])
                    pb = i % NP
                    tk.op("act", lambda: nc.scalar.activation(out=Pb[pb][:], in_=ps[:], func=AF.Exp, scale=SCALE),
                          reads=["ppS%d" % s], writes=["a_P%d" % pb])

                def pv_a(i):
                    qb, kt = steps[i]
                    pb = i % NP
                    tk.op("pe", [lambda: nc.tensor.matmul(po[:, 0:512], lhsT=va[:, kt, 0:128], rhs=Pb[pb][:, 0:512],
                                                          start=(kt == 0), stop=(kt == 63)),
                                 lambda: nc.tensor.matmul(po[:, 512:1024], lhsT=va[:, kt, 64:192], rhs=Pb[pb][:, 512:1024],
                                                          start=(kt == 0), stop=(kt == 63))],
                          reads=["a_P%d" % pb] + va_keys, writes=["ppO"])
                    if kt == 63:
                        normalize1(po, "ppO", Osb)
                        normalize2(attnT[0:64, 0:4, qb * 128:(qb + 1) * 128],
                                   attnT[64:128, 0:4, qb * 128:(qb + 1) * 128], R, Osb, Rl, False)

                qk_a(0)
                qk_a(1)
                for i in range(n):
                    if i + 2 < n:
                        qk_a(i + 2)
                    pv_a(i)

                for h in range(2):
                    for part, (slot_off, c0, c1) in enumerate([(0, 1024, 2048), (1, 0, 1024), (1, 1024, 2048), (2, 0, 1024)]):
                        tk.dma("sp", win_kb[h * 256:(h + 1) * 256, part * 1024:(part + 1) * 1024],
                               big_kb[h][bass.ds((j + slot_off) * 256, 256), c0:c1],
                               reads=["big_kb%d" % h, ("big_kb_pad", h, 0), ("big_kb_pad", h, 5)], writes=[("win_kb", h, part)])
                for p in range(4):
                    for qd in range(4):
                        tk.dma("sp", win_vb[qd * 1024:(qd + 1) * 1024, p * 192:(p + 1) * 192],
                               big_vb[p][bass.ds(j * T + 1024 + qd * 1024, 1024), :],
                               reads=["big_vb%d" % p, ("big_vb_pad", p, 0), ("big_vb_pad", p, 5)], writes=[("win_vb", p, qd)])

                def load_b(pair):
                    b = pair % 2
                    tk.dma("sp", kbw[b][:], win_kb[pair * 128:(pair + 1) * 128, :],
                           reads=[("win_kb", pair // 2, q_) for q_ in range(4)], writes=["a_kbw%d" % b])
                    for h in range(2):
                        r0 = 768 + h * 1280
                        tk.dma("sp", vbw[b][:, h * 10:(h + 1) * 10, :],
                               win_vb[r0:r0 + 1280, pair * 192:(pair + 1) * 192]
                               .rearrange("(n p) c -> p n c", p=128),
                               reads=[("win_vb", pair, q_) for q_ in range(4)], writes=[("a_vbw", b, h)])
                    wv = win_vb[:, pair * 192:(pair + 1) * 192].rearrange("(t p r) c -> p r t c", t=2, p=128, r=16)
                    for rg in range(4):
                        for t_ in range(2):
                            tk.dma("sp", vbs[b][:, rg * 4:(rg + 1) * 4, t_, :], wv[:, rg * 4:(rg + 1) * 4, t_, :],
                                   reads=[("win_vb", pair, q_) for q_ in range(4)], writes=[("a_vbs", b, rg, t_)])

                stepsb = []
                for pair in range(4):
                    stepsb += [("s2", pair, rg, t) for rg in range(4) for t in range(2)]
                    stepsb += [("s1", pair, qk, m) for qk in range(4) for m in range(NMASK)]
                nb_ = len(stepsb)

                def qk_b(i):
                    kind, pair, x1_, x2_ = stepsb[i]
                    b = pair % 2
                    s = i % 2
                    ps = pp[s]
                    pb = i % NP
                    if kind == "s1":
                        qk, m = x1_, x2_
                        wt = 6 + qk * 4 + m
                        fns = [lambda: nc.tensor.matmul(ps[:, 0:512], lhsT=kbw[b][0:64, wt * 128:(wt + 1) * 128],
                                                        rhs=qbT[0:64, pair, qk * 512:(qk + 1) * 512], start=True, stop=True),
                               lambda: nc.tensor.matmul(ps[:, 512:1024], lhsT=kbw[b][64:128, wt * 128:(wt + 1) * 128],
                                                        rhs=qbT[64:128, pair, qk * 512:(qk + 1) * 512], start=True, stop=True)]
                        mk = maskb[:, m, :].unsqueeze(1).broadcast_to([128, 2, 512])
                        pv_ = Pb[pb][:].rearrange("p (h q) -> p h q", h=2)
                        mkey = "a_mask"
                    else:
                        rg, t = x1_, x2_
                        fns = []
                        for rs in range(4):
                            r = rg * 4 + rs
                            for h in range(2):
                                c0 = (h * 4 + rs) * 128
                                k0 = 2048 * t + r
                                fns.append(lambda h=h, c0=c0, k0=k0, r=r: nc.tensor.matmul(
                                    ps[:, c0:c0 + 128], lhsT=kbw[b][h * 64:(h + 1) * 64, k0:k0 + 2033:16],
                                    rhs=qbT[h * 64:(h + 1) * 64, pair, r:2048:16], start=True, stop=True))
                        mk = mk2[:, t, :].unsqueeze(1).broadcast_to([128, 8, 128])
                        pv_ = Pb[pb][:].rearrange("p (h q) -> p h q", h=8)
                        mkey = "a_mk2"
                    tk.op("pe", fns, reads=["a_kbw%d" % b, "qbT"], writes=["ppS%d" % s])
                    tk.op("act", lambda: nc.scalar.activation(out=Pb[pb][:], in_=ps[:], func=AF.Exp, scale=SCALE),
                          reads=["ppS%d" % s], writes=["a_P%d" % pb])
                    tk.op("dve", lambda: nc.vector.tensor_tensor(out=pv_, in0=pv_, in1=mk, op=ALU.mult),
                          reads=["a_P%d" % pb, mkey], writes=["a_P%d" % pb])

                def pv_b(i):
                    kind, pair, x1_, x2_ = stepsb[i]
                    b = pair % 2
                    pb = i % NP
                    if kind == "s1":
                        qk, m = x1_, x2_
                        wt = 6 + qk * 4 + m
                        poX, okeyX = (pp[2], "ppS2") if qk % 2 == 1 else (po, "ppO")
                        tk.op("pe", [lambda: nc.tensor.matmul(poX[:, 0:512], lhsT=vbw[b][:, wt - 6, 0:128], rhs=Pb[pb][:, 0:512],
                                                              start=(m == 0), stop=(m == NMASK - 1)),
                                     lambda: nc.tensor.matmul(poX[:, 512:1024], lhsT=vbw[b][:, wt - 6, 64:192], rhs=Pb[pb][:, 512:1024],
                                                              start=(m == 0), stop=(m == NMASK - 1))],
                              reads=["a_P%d" % pb, ("a_vbw", b, 0), ("a_vbw", b, 1)], writes=[okeyX])
                        if m == NMASK - 1:
                            for h in range(2):
                                dstv = Osb[:, h * 512:(h + 1) * 512].rearrange("p (i g s) -> p i g s", i=32, g=4, s=4)
                                pov = poX[:, h * 512:(h + 1) * 512].rearrange("p (i g s) -> p i g s", i=32, g=4, s=4)
                                srcv = O2sb[:, :, h * 512:(h + 1) * 512].rearrange("p g (s i) -> p g s i", s=4)[
                                    :, :, :, 32 * qk:32 * qk + 32].rearrange("p g s i -> p i g s")
                                tk.op("dve", lambda dstv=dstv, pov=pov, srcv=srcv: nc.vector.tensor_tensor(out=dstv, in0=pov, in1=srcv, op=ALU.add),
                                      reads=[okeyX] + [("O2sb", g_) for g_ in range(4)], writes=["Osb"])
                            pend.append((i + 5, (attnT[0:64, 4 + pair, qk * 512:(qk + 1) * 512],
                                                 attnT[64:128, 4 + pair, qk * 512:(qk + 1) * 512])))
                    else:
                        rg, t = x1_, x2_
                        po2, okey2 = (pp[2], "ppS2") if rg % 2 == 1 else (po, "ppO")
                        fns = []
                        for rs in range(4):
                            r = rg * 4 + rs
                            for h in range(2):
                                c0 = (h * 4 + rs) * 128
                                fns.append(lambda h=h, c0=c0, r=r, rs=rs: nc.tensor.matmul(
                                    po2[:, c0:c0 + 128], lhsT=vbs[b][:, r, t, 64 * h:64 * h + 128], rhs=Pb[pb][:, c0:c0 + 128],
                                    start=(t == 0 and rs == 0), stop=(t == 1), skip_group_check=True))
                        tk.op("pe", fns, reads=["a_P%d" % pb, ("a_vbs", b, rg, t)], writes=[okey2])
                        if t == 1:
                            tk.op("dve", lambda: nc.vector.tensor_copy(out=O2sb[:, rg, :], in_=po2[:]),
                                  reads=[okey2], writes=[("O2sb", rg)])

                pend = []
                load_b(0)
                load_b(1)
                qk_b(0)
                qk_b(1)
                def pv_step(j):
                    pv_b(j)
                    pair = stepsb[j][1]
                    last_of_pair = (j + 1 == nb_) or (stepsb[j + 1][1] != pair)
                    if last_of_pair and pair + 2 < 4:
                        load_b(pair + 2)

                for i in range(nb_):
                    if pend and pend[0][0] <= i:
                        d0_, d1_ = pend.pop(0)[1]
                        normalize2(d0_, d1_, R, Osb, Rl, True)
                    if i + 2 < nb_:
                        qk_b(i + 2)
                    if i >= 1:
                        pv_step(i - 1)
                pv_step(nb_ - 1)
                while pend:
                    d0_, d1_ = pend.pop(0)[1]
                    normalize2(d0_, d1_, R, Osb, Rl, True)

        def outproj_phase(after_wo=None):
            qaT, qbT, attnT = mid["qaT"], mid["qbT"], mid["attnT"]
            with contextlib.ExitStack() as fs:
                def fsb(name, shape, dt):
                    return fs.enter_context(nc.sbuf_tensor("o_" + name, list(shape), dt))
                wo = fsb("wo", [128, KC, D], BF16)
                gpost = fsb("gpost", [128, D], F32)
                ysb = [fsb("ysb%d" % i, [128, D], F32) for i in range(3)]
                scr = fsb("scr", [128, D], F32)
                st = fsb("st", [128, 16], F32)
                load_gain(gpost, 3, "o_gpost")
                tk.dma("pool", wo[:], wout_d.rearrange("(k p) n -> p k n", p=128), writes=["o_wo"])
                if after_wo is not None:
                    after_wo()
                def o_front(tt):
                    b = tt % 2
                    b3 = tt % 3
                    fns = []
                    for hf in range(2):
                        for k in range(KC):
                            fns.append(lambda k=k, hf=hf: nc.tensor.matmul(
                                pp[b][:, hf * 512:(hf + 1) * 512], lhsT=attnT[:, k, tt * 128:(tt + 1) * 128],
                                rhs=wo[:, k, hf * 512:(hf + 1) * 512], start=(k == 0), stop=(k == KC - 1)))
                    tk.op("pe", fns, reads=["attnT", "o_wo"], writes=["pp%d" % b])
                    ky = "o_ysb%d" % b3
                    tk.op("act", lambda: nc.scalar.activation(out=ysb[b3][:], in_=pp[b][:], func=AF.Copy),
                          reads=["pp%d" % b], writes=[ky])
                    tk.op("dve", lambda: nc.vector.tensor_tensor(out=scr[:], in0=ysb[b3][:], in1=ysb[b3][:], op=ALU.mult),
                          reads=[ky], writes=["o_scr"])
                    tk.op("dve", lambda: nc.vector.tensor_reduce(out=st[:, 3 * b3:3 * b3 + 1], in_=scr[:], axis=AX.X, op=ALU.add),
                          reads=["o_scr"], writes=["o_ss%d" % b3])
                    rstd_pool(st[:, 3 * b3:3 * b3 + 1], st[:, 3 * b3 + 1:3 * b3 + 2], st[:, 3 * b3 + 2:3 * b3 + 3], 1, 1.0 / D,
                              "o_ss%d" % b3, "o_r%d" % b3)

                def o_back(tt):
                    b3 = tt % 3
                    ky = "o_ysb%d" % b3
                    tk.op("dve", lambda: nc.vector.scalar_tensor_tensor(out=ysb[b3][:], in0=ysb[b3][:], scalar=st[:, 3 * b3 + 2:3 * b3 + 3],
                                                                       in1=gpost[:], op0=ALU.mult, op1=ALU.mult),
                          reads=[ky, "o_r%d" % b3, "o_gpost"], writes=[ky])
                    tk.dma("sp", x2_d[tt * 128:(tt + 1) * 128, :], ysb[b3][:], reads=[ky], writes=[("x2", tt)])

                for tt in range(NT + 1):
                    if tt < NT:
                        o_front(tt)
                    if tt >= 1:
                        o_back(tt - 1)

        if "ffn1" in phases:
            ffn_phase(0, x_d, "xin", x1_d, "x1", 0, 1)
        if "ffn1out" in phases:
            ffn_phase(0, x_d, "xin", out_d, "out", 0, 1)
        tk.barrier()
        with nc.sbuf_tensor("s_attnT", [128, 8, T], BF16) as attnT_:
            mid["attnT"] = attnT_
            with (nc.sbuf_tensor("s_qaT", [128, NT, 4, 128], BF16) as qaT_,
                  nc.sbuf_tensor("s_qbT", [128, 4, T], BF16) as qbT_,
                  nc.sbuf_tensor("s_kaT", [128, 4, T], BF16) as kaT_,
                  nc.sbuf_tensor("s_va", [128, 64, 192], BF16) as va_):
                mid["qaT"], mid["qbT"] = qaT_, qbT_
                mid["kaT"], mid["va"] = kaT_, va_
                if "qkv" in phases:
                    qkv_phase()
                tk.barrier()
                if "attn" in phases:
                    attn_phase()
                tk.barrier()
            mid["qaT"] = mid["qbT"] = mid["kaT"] = mid["va"] = None
            with (nc.sbuf_tensor("f1_wg", [128, KC, FF], BF16) as wg2_,
                  nc.sbuf_tensor("f1_wu", [128, KC, FF], BF16) as wu2_,
                  nc.sbuf_tensor("f1_wd", [128, FC, D], BF16) as wd2_):
                if "outproj" in phases:
                    outproj_phase((lambda: ffn_weights(1, wg2_, wu2_, wd2_)) if "ffn2" in phases else None)
                elif "ffn2" in phases:
                    ffn_weights(1, wg2_, wu2_, wd2_)
                tk.barrier()
                if "ffn2" in phases:
                    ffn_phase(1, x2_d, "x2", out_d, "out", 4, 5, pre=(wg2_, wu2_, wd2_, attnT_[:].bitcast(F32)))
                tk.barrier()

        for tt in range(NT):
            tk._wait("sp", tk.last_w.get(("out", tt)))
    return nc


_NC_CACHE = {}


def _rope_tables(j):
    pos = np.arange(j * T, (j + 1) * T, dtype=np.int64)
    row = (pos // 64).astype(np.float64)
    col = (pos % 64).astype(np.float64)
    fa = 10000.0 ** (-np.arange(0, 32, 2, dtype=np.float64) / 32.0)
    fb = 10000.0 ** (-np.arange(0, 64, 2, dtype=np.float64) / 64.0)
    ang_r = row[:, None] * fa[None, :]
    ang_c = col[:, None] * fa[None, :]
    ang_b = pos.astype(np.float64)[:, None] * fb[None, :]
    ca = np.concatenate([np.cos(ang_r), np.cos(ang_c)], axis=1)
    sa = np.concatenate([np.sin(ang_r), np.sin(ang_c)], axis=1)
    return np.concatenate([ca, sa, np.cos(ang_b), np.sin(ang_b)], axis=1).astype(np.float32)


def _mask_b():
    kk = np.arange(128)[:, None, None]
    m = np.arange(NMASK)[None, :, None]
    qq = np.arange(512)[None, None, :]
    d = 128 * m - 256 + kk - qq
    c = (np.abs(d) <= 64).astype(np.float32)
    c += ((d % 4 == 0) & (np.abs(d) <= 256)).astype(np.float32)
    return c.astype(ml_dtypes.bfloat16)


def _mask_16():
    k = np.arange(128)[:, None]
    i = np.arange(128)[None, :]
    return np.stack([(k >= i), (k <= i)], axis=1).astype(np.float32).astype(ml_dtypes.bfloat16)


def kernel(x, ffn1_pre_g, ffn1_post_g, ffn1_w_gate, ffn1_w_up, ffn1_w_down,
           mix_pre_g, mix_post_g, w_qkv, q_norm_g, k_norm_g, w_out,
           ffn2_pre_g, ffn2_post_g, ffn2_w_gate, ffn2_w_up, ffn2_w_down):
    f = lambda a: np.ascontiguousarray(np.asarray(a, dtype=np.float32))
    x = f(x)
    hq = [0, 4, 1, 5, 2, 6, 3, 7]
    qa_cols = np.concatenate([np.arange(h * 64, (h + 1) * 64) for h in hq])
    cols = np.concatenate([qa_cols, np.arange(512, 2304)])
    wqkv = f(np.asarray(w_qkv)[0][:, cols])
    rows = np.concatenate([qa_cols, np.arange(512, 1024)])
    wout = f(np.asarray(w_out)[0][rows, :])
    gains = f(np.stack([np.asarray(g)[0] for g in (ffn1_pre_g, ffn1_post_g, mix_pre_g, mix_post_g, ffn2_pre_g, ffn2_post_g)]))
    gqk = f(np.stack([np.asarray(q_norm_g)[0], np.asarray(k_norm_g)[0]]))
    shared = {
        "wg1": f(np.asarray(ffn1_w_gate)[0]), "wu1": f(np.asarray(ffn1_w_up)[0]), "wd1": f(np.asarray(ffn1_w_down)[0]),
        "wg2": f(np.asarray(ffn2_w_gate)[0]), "wu2": f(np.asarray(ffn2_w_up)[0]), "wd2": f(np.asarray(ffn2_w_down)[0]),
        "wqkv": wqkv, "wout": wout, "gains": gains, "gqk": gqk,
        "maskb": _mask_b(), "mk2": _mask_16(), "ident": np.eye(128, dtype=np.float32).astype(ml_dtypes.bfloat16),
    }
    ropes = [_rope_tables(j) for j in range(4)]
    in_maps = []
    for c in range(NCORES):
        b, j = c // 4, c % 4
        m = dict(shared)
        m["x"] = np.ascontiguousarray(x[b, j * T:(j + 1) * T, :])
        m["rope"] = ropes[j]
        in_maps.append(m)
    if "nc" not in _NC_CACHE:
        _NC_CACHE["nc"] = build_nc()
    res = run_bass_kernel_spmd(_NC_CACHE["nc"], in_maps, core_ids=list(range(NCORES)))
    out = np.empty((2, SEQ, D), dtype=np.float32)
    for c in range(NCORES):
        b, j = c // 4, c % 4
        out[b, j * T:(j + 1) * T, :] = np.asarray(res.results[c]["out"], dtype=np.float32)
    return out
```
